# Optimizing a Trainium2 kernel written in Bass

```python
import math
import jax
import jax.numpy as jnp
from jax import lax
import numpy as np

D_MODEL = 1024
BATCH = 2
SEQ = 8192
DEPTH = 2

GRID_W = 64
CTX_LEN = 256
N_HEADS_A = 4
HEAD_DIM_A = 64
V_DIM_A = 2 * HEAD_DIM_A
ATTN_WIDTH = N_HEADS_A * V_DIM_A
SSM_WIDTH = D_MODEL // 2
SSM_GROUP = 16
SSM_GROUPS = SSM_WIDTH // SSM_GROUP
SSM_STATE = 64
D_FF = 4 * D_MODEL
Q_COLS = N_HEADS_A * 2 * HEAD_DIM_A
K_COLS = N_HEADS_A * 2 * HEAD_DIM_A
V_COLS = ATTN_WIDTH
U_COLS = SSM_WIDTH
G_COLS = 2 * D_MODEL
IN_SPLITS = (Q_COLS, Q_COLS + K_COLS, Q_COLS + K_COLS + V_COLS, Q_COLS + K_COLS + V_COLS + U_COLS)
IN_COLS = Q_COLS + K_COLS + V_COLS + U_COLS + G_COLS
Q_BLOCK = 128
ROPE_BASE = 10000.0
ROPE_AXIS_DIM = HEAD_DIM_A // 2
N_FREQ = ROPE_AXIS_DIM // 2
NORM_EPS = 1e-6

kernel_name = "hybrid_diffattn_s5_dit_trunk"


def rmsnorm(x, g):
    x32 = x.astype(jnp.float32)
    y = x32 * lax.rsqrt(jnp.mean(x32 * x32, axis=-1, keepdims=True) + NORM_EPS)
    return (y * g.astype(jnp.float32)).astype(x.dtype)


def axial_rope_tables(n_tokens):
    rows = n_tokens // GRID_W
    row = jnp.repeat(jnp.arange(rows, dtype=jnp.int32), GRID_W)
    col = jnp.tile(jnp.arange(GRID_W, dtype=jnp.int32), rows)
    inv_freq = ROPE_BASE ** (-jnp.arange(N_FREQ, dtype=jnp.float32) / N_FREQ)
    ang = jnp.stack([row.astype(jnp.float32)[:, None] * inv_freq,
                     col.astype(jnp.float32)[:, None] * inv_freq], axis=1)
    return jnp.cos(ang), jnp.sin(ang)


def apply_axial_rope(t, cos, sin):
    xs = t.reshape(t.shape[:-1] + (2, 2, N_FREQ))
    x1, x2 = xs[..., 0, :], xs[..., 1, :]
    cs = cos[None, :, None, None].astype(t.dtype)
    sn = sin[None, :, None, None].astype(t.dtype)
    out = jnp.stack([x1 * cs - x2 * sn, x2 * cs + x1 * sn], axis=-2)
    return out.reshape(t.shape)


def diff_attend(q, k, v, lam):
    s = jnp.einsum('bqhcd,bkhcd->bhcqk', q, k).astype(jnp.float32) * (HEAD_DIM_A ** -0.5)
    p = jax.nn.softmax(s, axis=-1)
    pd = p[:, :, 0] - lam * p[:, :, 1]
    return jnp.einsum('bhqk,bkhe->bqhe', pd.astype(v.dtype), v)


def latent_diff_attention(q, k_all, v_all, lam):
    b, l = q.shape[:2]
    nb = l // Q_BLOCK
    qb = q.reshape((b, nb, Q_BLOCK) + q.shape[2:]).swapaxes(0, 1)
    ob = lax.map(lambda qi: diff_attend(qi, k_all, v_all, lam), qb)
    return ob.swapaxes(0, 1).reshape(b, l, N_HEADS_A, V_DIM_A)


def s5_discretize(a_re, a_im, b_re, b_im, log_dt):
    lam = lax.complex(a_re.astype(jnp.float32), a_im.astype(jnp.float32))
    dt = jnp.exp(log_dt.astype(jnp.float32))[:, None]
    lam_bar = jnp.exp(lam * dt)
    bmat = lax.complex(b_re.astype(jnp.float32), b_im.astype(jnp.float32))
    b_bar = ((lam_bar - 1.0) / lam)[..., None] * bmat
    return lam_bar, b_bar


def _ssm_combine(e_i, e_j):
    a_i, b_i = e_i
    a_j, b_j = e_j
    return a_j * a_i, a_j * b_i + b_j


def s5_scan(u4, lam_bar, b_bar, h0, reverse):
    bu = jnp.einsum('gph,blgh->blgp', b_bar, u4.astype(jnp.complex64))
    edge = -1 if reverse else 0
    bu = bu.at[:, edge].add(lam_bar * h0)
    a = jnp.broadcast_to(lam_bar, bu.shape)
    _, xs = lax.associative_scan(_ssm_combine, (a, bu), axis=1, reverse=reverse)
    return xs


def s5_readout(cmat, xs):
    return jnp.real(jnp.einsum('ghp,blgp->blgh', cmat, xs))


def s5_bidirectional(u, uc, a_re, a_im, b_re, b_im, c_re, c_im, log_dt, ssm_d, ctx_out):
    b, l, _ = u.shape
    lc = uc.shape[1]
    d32 = ssm_d.astype(jnp.float32)
    u32 = u.astype(jnp.float32)
    uc32 = uc.astype(jnp.float32)
    u4 = u32.reshape(b, l, SSM_GROUPS, SSM_GROUP)
    uc4 = uc32.reshape(b, lc, SSM_GROUPS, SSM_GROUP)
    y = u32 * d32
    yc = uc32 * d32 if ctx_out else None
    for direction, rev in ((0, False), (1, True)):
        lam_bar, b_bar = s5_discretize(a_re[direction], a_im[direction], b_re[direction],
                                       b_im[direction], log_dt[direction])
        cmat = lax.complex(c_re[direction].astype(jnp.float32), c_im[direction].astype(jnp.float32))
        h0 = jnp.zeros((b, SSM_GROUPS, SSM_STATE), jnp.complex64)
        xs_c = s5_scan(uc4, lam_bar, b_bar, h0, rev)
        h_ctx = xs_c[:, 0] if rev else xs_c[:, -1]
        xs_l = s5_scan(u4, lam_bar, b_bar, h_ctx, rev)
        y = y + s5_readout(cmat, xs_l).reshape(b, l, SSM_WIDTH)
        if ctx_out:
            yc = yc + s5_readout(cmat, xs_c).reshape(b, lc, SSM_WIDTH)
    return y.astype(u.dtype), (yc.astype(uc.dtype) if ctx_out else None)


def ssm_glu(y, w_glu, b_glu):
    y = jax.nn.gelu(y)
    return y * jax.nn.sigmoid(y @ w_glu + b_glu)


def sq_relu_mlp(h, w1, b1, w2, b2):
    return jnp.square(jax.nn.relu(h @ w1 + b1)) @ w2 + b2


def trunk_layer(x, xc, c, c_ctx, cos, sin, lam_init, ctx_out,
                ada_w, ada_b, norm_g, w_in, gate_b, lam_qk, subln_g, w_br_a,
                a_re, a_im, b_re, b_im, c_re, c_im, log_dt, ssm_d,
                w_glu, b_glu, w_br_s, w_out, w_mlp1, b_mlp1, w_mlp2, b_mlp2):
    b, l, _ = x.shape
    lc = xc.shape[1]
    sh1, sc1, gt1, sh2, sc2, gt2 = [m[:, None, :] for m in
                                    jnp.split(jax.nn.silu(c) @ ada_w + ada_b, 6, axis=-1)]
    csh1, csc1, cgt1, csh2, csc2, cgt2 = jnp.split(jax.nn.silu(c_ctx) @ ada_w + ada_b, 6, axis=-1)

    h = rmsnorm(x, norm_g[0]) * (1 + sc1) + sh1
    hc = rmsnorm(xc, norm_g[0]) * (1 + csc1) + csh1
    q, k, v, u, g = jnp.split(h @ w_in, IN_SPLITS, axis=-1)
    if ctx_out:
        qc, kc, vc, uc, gc = jnp.split(hc @ w_in, IN_SPLITS, axis=-1)
    else:
        kc, vc, uc = jnp.split(hc @ w_in[:, Q_COLS:IN_SPLITS[3]], (K_COLS, K_COLS + V_COLS), axis=-1)

    lam32 = lam_qk.astype(jnp.float32)
    lam = jnp.exp(jnp.sum(lam32[0] * lam32[1])) - jnp.exp(jnp.sum(lam32[2] * lam32[3])) + lam_init
    qk_shape = (N_HEADS_A, 2, HEAD_DIM_A)
    q = apply_axial_rope(q.reshape((b, l) + qk_shape), cos, sin)
    k = apply_axial_rope(k.reshape((b, l) + qk_shape), cos, sin)
    v = v.reshape(b, l, N_HEADS_A, V_DIM_A)
    kc = kc.reshape((b, lc) + qk_shape)
    vc = vc.reshape(b, lc, N_HEADS_A, V_DIM_A)
    k_all = jnp.concatenate([kc, k], axis=1)
    v_all = jnp.concatenate([vc, v], axis=1)
    o = latent_diff_attention(q, k_all, v_all, lam)
    pa = (rmsnorm(o, subln_g) * (1.0 - lam_init)).reshape(b, l, ATTN_WIDTH) @ w_br_a

    y_lat, y_ctx = s5_bidirectional(u, uc, a_re, a_im, b_re, b_im, c_re, c_im, log_dt, ssm_d, ctx_out)
    ps = ssm_glu(y_lat, w_glu, b_glu) @ w_br_s

    ga, gs = jnp.split(jax.nn.sigmoid(g + gate_b), 2, axis=-1)
    mix = (ga * pa + gs * ps) @ w_out
    x = x + gt1 * rmsnorm(mix, norm_g[1])

    h2 = rmsnorm(x, norm_g[2]) * (1 + sc2) + sh2
    x = x + gt2 * rmsnorm(sq_relu_mlp(h2, w_mlp1, b_mlp1, w_mlp2, b_mlp2), norm_g[3])

    if not ctx_out:
        return x, None

    oc = diff_attend(qc.reshape((b, lc) + qk_shape), kc, vc, lam)
    pac = (rmsnorm(oc, subln_g) * (1.0 - lam_init)).reshape(b, lc, ATTN_WIDTH) @ w_br_a
    psc = ssm_glu(y_ctx, w_glu, b_glu) @ w_br_s
    gac, gsc = jnp.split(jax.nn.sigmoid(gc + gate_b), 2, axis=-1)
    mixc = (gac * pac + gsc * psc) @ w_out
    xc = xc + cgt1 * rmsnorm(mixc, norm_g[1])
    hc2 = rmsnorm(xc, norm_g[2]) * (1 + csc2) + csh2
    xc = xc + cgt2 * rmsnorm(sq_relu_mlp(hc2, w_mlp1, b_mlp1, w_mlp2, b_mlp2), norm_g[3])
    return x, xc


def setup_inputs(seed: int = 0) -> dict:
    key = jax.random.key(seed)
    ks = iter(jax.random.split(key, 40))

    def nrm(shape, std):
        return jax.random.normal(next(ks), shape, jnp.float32) * std

    D, G, P, Hg = D_MODEL, SSM_GROUPS, SSM_STATE, SSM_GROUP
    log_lo, log_hi = math.log(1e-3), math.log(1e-1)
    return {
        "x": nrm((BATCH, SEQ, D), 1.0),
        "c": nrm((BATCH, D), 1.0),
        "ctx": nrm((BATCH, CTX_LEN, D), 1.0),
        "c_ctx": nrm((D,), 1.0),
        "ada_w": nrm((DEPTH, D, 6 * D), 0.5 * D ** -0.5),
        "ada_b": nrm((DEPTH, 6 * D), 0.02),
        "norm_g": 1.0 + nrm((DEPTH, 4, D), 0.02),
        "w_in": nrm((DEPTH, D, IN_COLS), D ** -0.5),
        "gate_b": nrm((DEPTH, G_COLS), 0.02),
        "lam_qk": nrm((DEPTH, 4, HEAD_DIM_A), 0.1),
        "subln_g": 1.0 + nrm((DEPTH, V_DIM_A), 0.02),
        "w_br_a": nrm((DEPTH, ATTN_WIDTH, D), ATTN_WIDTH ** -0.5),
        "ssm_a_re": -0.5 * jnp.exp(nrm((DEPTH, 2, G, P), 0.05)),
        "ssm_a_im": jnp.pi * jnp.arange(P, dtype=jnp.float32) + nrm((DEPTH, 2, G, P), 0.01),
        "ssm_b_re": nrm((DEPTH, 2, G, P, Hg), (2 * Hg) ** -0.5),
        "ssm_b_im": nrm((DEPTH, 2, G, P, Hg), (2 * Hg) ** -0.5),
        "ssm_c_re": nrm((DEPTH, 2, G, Hg, P), (2 * P) ** -0.5),
        "ssm_c_im": nrm((DEPTH, 2, G, Hg, P), (2 * P) ** -0.5),
        "ssm_log_dt": log_lo + (log_hi - log_lo) * jax.random.uniform(next(ks), (DEPTH, 2, G), jnp.float32),
        "ssm_d": nrm((DEPTH, SSM_WIDTH), 0.5),
        "w_glu": nrm((DEPTH, SSM_WIDTH, SSM_WIDTH), SSM_WIDTH ** -0.5),
        "b_glu": nrm((DEPTH, SSM_WIDTH), 0.02),
        "w_br_s": nrm((DEPTH, SSM_WIDTH, D), SSM_WIDTH ** -0.5),
        "w_out": nrm((DEPTH, D, D), D ** -0.5),
        "w_mlp1": nrm((DEPTH, D, D_FF), D ** -0.5),
        "b_mlp1": nrm((DEPTH, D_FF), 0.02),
        "w_mlp2": nrm((DEPTH, D_FF, D), D_FF ** -0.5),
        "b_mlp2": nrm((DEPTH, D), 0.02),
    }


def reference(x, c, ctx, c_ctx, ada_w, ada_b, norm_g, w_in, gate_b, lam_qk, subln_g, w_br_a,
              ssm_a_re, ssm_a_im, ssm_b_re, ssm_b_im, ssm_c_re, ssm_c_im, ssm_log_dt, ssm_d,
              w_glu, b_glu, w_br_s, w_out, w_mlp1, b_mlp1, w_mlp2, b_mlp2):
    cos, sin = axial_rope_tables(x.shape[1])
    xc = ctx
    for i in range(DEPTH):
        lam_init = 0.8 - 0.6 * math.exp(-0.3 * i)
        x, xc = trunk_layer(
            x, xc, c, c_ctx, cos, sin, lam_init, i < DEPTH - 1,
            ada_w[i], ada_b[i], norm_g[i], w_in[i], gate_b[i], lam_qk[i], subln_g[i], w_br_a[i],
            ssm_a_re[i], ssm_a_im[i], ssm_b_re[i], ssm_b_im[i], ssm_c_re[i], ssm_c_im[i],
            ssm_log_dt[i], ssm_d[i], w_glu[i], b_glu[i], w_br_s[i], w_out[i],
            w_mlp1[i], b_mlp1[i], w_mlp2[i], b_mlp2[i])
    return x
```

```python
import math
from contextlib import ExitStack
import numpy as np
import ml_dtypes
import concourse.bass as bass
import concourse.mybir as mybir
from concourse.bass_utils import run_bass_kernel_spmd

F32 = mybir.dt.float32
BF16 = mybir.dt.bfloat16
ALU = mybir.AluOpType
AF = mybir.ActivationFunctionType
NDMA = 12
ENGS = ("sp", "act", "dve", "pool", "pe")

D = 1024
SEQ = 8192
CTX = 256
NTOK = SEQ + CTX
NT = 2048 + 64
EPS = 1e-6
TWO_PI = 6.283185307179586
INV_2PI = 1.0 / TWO_PI
MAGIC = 12582912.0
PI_LO = 3.1415925
T_CHUNKS = [(0, 512), (512, 512), (1024, 512), (1536, 512), (2048, 64)]


class Res:
    __slots__ = ("w", "r")

    def __init__(self):
        self.w = None
        self.r = []


class Builder:
    def __init__(self, nc, es):
        self.nc = nc
        self.semh = {}
        for e in ENGS:
            self.semh[e] = es.enter_context(nc.semaphore("s_" + e))
        for i in range(NDMA):
            self.semh[("dma", i)] = es.enter_context(nc.semaphore("s_dma%d" % i))
        self.semh["cc"] = es.enter_context(nc.semaphore("s_cc"))
        self.pidcache = {}
        self.ccnt = 0
        self.cnt = {e: 0 for e in ENGS}
        self.dcnt = [0] * NDMA
        self.ndma = 0
        self.seen = {e: {} for e in ENGS}
        self.plan = {e: [] for e in ENGS}
        self.nops = 0

    def _deps(self, eng, reads, writes):
        deps = {}
        same_raw = 0
        for r in reads:
            if r.w is not None:
                k, v = r.w
                if deps.get(k, 0) < v:
                    deps[k] = v
                if k == eng and v > same_raw:
                    same_raw = v
        for w in writes:
            if w.w is not None:
                k, v = w.w
                if deps.get(k, 0) < v:
                    deps[k] = v
            for (k, v) in w.r:
                if deps.get(k, 0) < v:
                    deps[k] = v
        waits = []
        seen = self.seen[eng]
        for k, v in deps.items():
            if k == eng:
                continue
            if seen.get(k, 0) < v:
                seen[k] = v
                waits.append((k, v))
        if eng in ("act", "dve", "pool") and same_raw > seen.get(eng, 0):
            seen[eng] = same_raw
            waits.append((eng, same_raw))
        return waits

    def op(self, eng, fn, reads=(), writes=(), dma=False, cc=False):
        waits = self._deps(eng, reads, writes)
        if cc:
            self.ccnt += 1
            tok = ("cc", self.ccnt)
        elif dma:
            i = self.ndma % NDMA
            self.ndma += 1
            self.dcnt[i] += 16
            tok = (("dma", i), self.dcnt[i])
        else:
            self.cnt[eng] += 1
            tok = (eng, self.cnt[eng])
        self.plan[eng].append((waits, fn, tok, dma))
        for r in reads:
            r.r.append(tok)
        for w in writes:
            w.w = tok
            w.r = []
        self.nops += 1
        return tok

    def dma(self, out, in_, reads=(), writes=(), eng="sp", slow=False):
        if slow:
            return self.op(eng, lambda h: h.dma_start(out=out, in_=in_, allow_slow_non_contiguous=True),
                           reads, writes, dma=True)
        return self.op(eng, lambda h: h.dma_start(out=out, in_=in_), reads, writes, dma=True)

    def barrier(self, eng, ress):
        waits = self._deps(eng, ress, ())
        self.plan[eng].append((waits, None, None, False))

    def flush(self):
        nc = self.nc
        plan = self.plan
        semh = self.semh

        self.pidcache = {}

        def replay(eng):
            def run(h):
                for (waits, fn, tok, dma) in plan[eng]:
                    for (k, v) in waits:
                        h.wait_ge(semh[k], v)
                    if fn is None:
                        continue
                    ins = fn(h)
                    ins.then_inc(semh[tok[0]], 16 if dma else 1)
            return run

        with nc.Block() as block:
            block.sync(replay("sp"))
            block.scalar(replay("act"))
            block.vector(replay("dve"))
            block.gpsimd(replay("pool"))
            block.tensor(replay("pe"))
        self.plan = {e: [] for e in ENGS}


_UNIQ = [0]


def sb(nc, es, name, shape, dt):
    _UNIQ[0] += 1
    return es.enter_context(nc.sbuf_tensor("%s_%d" % (name, _UNIQ[0]), list(shape), dt))


def ps(nc, es, name, shape, dt=F32):
    _UNIQ[0] += 1
    return es.enter_context(nc.psum_tensor("%s_%d" % (name, _UNIQ[0]), list(shape), dt))


class Consts:
    pass


DBG = {"on": False, "skip_attn": False, "nc": None}


def dump(B, name, ap, shape, dt, reads):
    if not DBG["on"]:
        return
    o = dram_out(DBG["nc"], "z_" + name, shape, dt)
    r = Res()
    B.dma(o, ap, reads=reads, writes=[r])
    B.barrier("sp", [r])


def make_consts(B, nc, es):
    C = Consts()
    C.ones_bf = sb(nc, es, "c_ones", [128, 128], BF16)
    C.ident = sb(nc, es, "c_ident", [128, 128], F32)
    C.ones_f = sb(nc, es, "c_onesf", [128, 128], F32)
    C.eps = sb(nc, es, "c_eps", [128, 1], F32)
    C.iota = sb(nc, es, "c_iota", [128, 512], F32)
    C.r = Res()
    B.op("pool", lambda h: h.memset(C.ones_bf[:], 1.0), writes=[C.r])
    B.op("pool", lambda h: h.memset(C.eps[:], EPS), writes=[C.r])
    B.op("pool", lambda h: h.memset(C.ones_f[:], 1.0), writes=[C.r])
    B.op("pool", lambda h: h.memset(C.ident[:], 0.0), writes=[C.r])
    B.op("pool", lambda h: h.affine_select(C.ident[:], C.ident[:], pattern=[[-1, 128]],
                                           compare_op=ALU.not_equal, fill=1.0, base=0, channel_multiplier=1),
         reads=[C.r], writes=[C.r])
    B.op("pool", lambda h: h.iota(C.iota[:], pattern=[[1, 512]], base=0, channel_multiplier=0,
                                  allow_small_or_imprecise_dtypes=True), writes=[C.r])
    return C


def range_reduce(B, eng, out, x, tmp, rin, rout, add_half_pi=False):
    if add_half_pi:
        B.op(eng, lambda h: h.tensor_scalar(out, x, math.pi / 2, None, ALU.add), reads=rin, writes=rout)
        x = out
        rin = rout
    B.op(eng, lambda h: h.tensor_scalar(tmp, x, INV_2PI, MAGIC, ALU.mult, ALU.add), reads=rin, writes=rout)
    B.op(eng, lambda h: h.tensor_scalar(tmp, tmp, MAGIC, -TWO_PI, ALU.subtract, ALU.mult), reads=rout, writes=rout)
    B.op(eng, lambda h: h.tensor_tensor(out, tmp, x, ALU.add), reads=list(rin) + list(rout), writes=rout)
    B.op(eng, lambda h: h.tensor_scalar(out, out, PI_LO, -PI_LO, ALU.min, ALU.max), reads=rout, writes=rout)


def emit_mods(B, nc, es, C, md, md_r, c_b, c_ctx, ada_w, ada_b, norm_g, tag):
    with ExitStack() as ls:
        cT = sb(nc, ls, tag + "cT", [128, 8, 2], F32)
        sT = sb(nc, ls, tag + "sT", [128, 8, 2], F32)
        adab = sb(nc, ls, tag + "adab", [128, 48], F32)
        gn = sb(nc, ls, tag + "gn", [128, 4, 8], F32)
        mods = sb(nc, ls, tag + "mods", [128, 48, 2], F32)
        wbuf = [sb(nc, ls, tag + "aw%d" % i, [128, 8, 768], F32) for i in range(2)]
        mps = ps(nc, ls, tag + "mps", [128, 48, 2])
        r_c, r_s, r_ab, r_g, r_m, r_ps = Res(), Res(), Res(), Res(), Res(), Res()
        r_w = [Res(), Res()]
        B.dma(cT[:, :, 0], c_b.rearrange("(k p) -> p k", p=128), writes=[r_c], slow=True)
        B.dma(cT[:, :, 1], c_ctx.rearrange("(k p) -> p k", p=128), writes=[r_c], slow=True)
        B.dma(adab[:], ada_b.rearrange("(c p) -> p c", p=128), writes=[r_ab], slow=True)
        B.dma(gn[:], norm_g.rearrange("r (k p) -> p r k", p=128), writes=[r_g], slow=True)
        B.op("act", lambda h: h.activation(sT[:], cT[:], AF.Silu), reads=[r_c], writes=[r_s])
        awv = ada_w.rearrange("(k p) n -> p k n", p=128)
        for cc in range(8):
            wb = wbuf[cc % 2]
            B.dma(wb[:], awv[:, :, cc * 768:(cc + 1) * 768], writes=[r_w[cc % 2]])

            def mm(h, wb=wb, cc=cc):
                ins = None
                for ct in range(6):
                    for k in range(8):
                        ins = h.matmul(mps[:, cc * 6 + ct, :], wb[:, k, ct * 128:(ct + 1) * 128], sT[:, k, :],
                                       start=(k == 0), stop=(k == 7))
                return ins
            B.op("pe", mm, reads=[r_w[cc % 2], r_s], writes=[r_ps])
        for s in range(2):
            B.op("dve", lambda h, s=s: h.tensor_tensor(mods[:, :, s], mps[:, :, s], adab[:], ALU.add),
                 reads=[r_ps, r_ab], writes=[r_m])
        for s in range(2):
            B.op("dve", lambda h, s=s: h.tensor_copy(md[:, 0, :, s], mods[:, 0:8, s]), reads=[r_m], writes=[md_r])
            B.op("dve", lambda h, s=s: h.tensor_copy(md[:, 3, :, s], mods[:, 24:32, s]), reads=[r_m], writes=[md_r])
            B.op("dve", lambda h, s=s: h.scalar_tensor_tensor(md[:, 1, :, s], mods[:, 8:16, s], 1.0, gn[:, 0, :],
                                                               ALU.add, ALU.mult), reads=[r_m, r_g], writes=[md_r])
            B.op("dve", lambda h, s=s: h.scalar_tensor_tensor(md[:, 4, :, s], mods[:, 32:40, s], 1.0, gn[:, 2, :],
                                                               ALU.add, ALU.mult), reads=[r_m, r_g], writes=[md_r])
            B.op("dve", lambda h, s=s: h.tensor_tensor(md[:, 2, :, s], mods[:, 16:24, s], gn[:, 1, :], ALU.mult),
                 reads=[r_m, r_g], writes=[md_r])
            B.op("dve", lambda h, s=s: h.tensor_tensor(md[:, 5, :, s], mods[:, 40:48, s], gn[:, 3, :], ALU.mult),
                 reads=[r_m, r_g], writes=[md_r])
        B.flush()


def emit_rstd(B, C, x, n, sq, ssps, sd, rstd, r_x, r_sq, r_ps, r_sd, r_rstd, nk=8, denom=1024.0):
    B.op("act", lambda h: h.activation(sq[:, 0:nk, :n], x[:, 0:nk, :n], AF.Square), reads=[r_x], writes=[r_sq])

    def mm(h):
        ins = None
        for k in range(nk):
            ins = h.matmul(ssps[:, :n], C.ones_bf[:], sq[:, k, :n], start=(k == 0), stop=(k == nk - 1))
        return ins
    B.op("pe", mm, reads=[r_sq, C.r], writes=[r_ps])
    B.op("act", lambda h: h.activation(sd[:, :n], ssps[:, :n], AF.Sqrt, bias=C.eps[:], scale=1.0 / denom),
         reads=[r_ps, C.r], writes=[r_sd])
    B.op("dve", lambda h: h.reciprocal(rstd[:, :n], sd[:, :n]), reads=[r_sd], writes=[r_rstd])


def emit_norm_mod(B, x, n, rstd, gain, shift, out, tmp, r_x, r_rstd, r_md, r_tmp, r_out):
    for k in range(8):
        B.op("dve", lambda h, k=k: h.scalar_tensor_tensor(tmp[:, k % 2, :n], x[:, k, :n], gain[:, k:k + 1], rstd[:, :n],
                                                           ALU.mult, ALU.mult),
             reads=[r_x, r_rstd, r_md], writes=[r_tmp[k % 2]])
        B.op("act", lambda h, k=k: h.activation(out[:, k, :n], tmp[:, k % 2, :n], AF.Identity, bias=shift[:, k:k + 1]),
             reads=[r_tmp[k % 2], r_md], writes=[r_out])


def phase_T0(B, nc, es, C, io, defer=False):
    with ExitStack() as ls_own:
        ls = es if defer else ls_own
        if "md" in io:
            md, md_r = io["md"], io["md_r"]
        else:
            md = sb(nc, ls, "t0_md", [128, 6, 8, 2], F32)
            md_r = Res()
            emit_mods(B, nc, ls, C, md, md_r, io["c_b"], io["c_ctx"], io["ada_w"], io["ada_b"], io["norm_g"], "t0m_")
        xb = [sb(nc, ls, "t0_x%d" % i, [128, 8, 512], F32) for i in range(2)]
        hb = [sb(nc, ls, "t0_h%d" % i, [128, 8, 512], BF16) for i in range(2)]
        sq = sb(nc, ls, "t0_sq", [128, 8, 512], BF16)
        tmp = sb(nc, ls, "t0_tmp", [128, 2, 512], F32)
        sd = sb(nc, ls, "t0_sd", [128, 512], F32)
        rstd = sb(nc, ls, "t0_rstd", [128, 512], F32)
        ssps = ps(nc, ls, "t0_ss", [128, 512])
        r_x = [Res(), Res()]
        r_h = [Res(), Res()]
        r_sq, r_ps, r_sd, r_rstd, r_tmp = Res(), Res(), Res(), Res(), [Res(), Res()]
        xv = io["xT"].rearrange("(k p) n -> p k n", p=128)
        hv = io["hT_out"].rearrange("(k p) n -> p k n", p=128)
        for ci, (t0, n) in enumerate(T_CHUNKS):
            s = 1 if ci == 4 else 0
            x = xb[ci % 2]
            hh = hb[ci % 2]
            B.dma(x[:, :, :n], xv[:, :, t0:t0 + n], writes=[r_x[ci % 2]])
            emit_rstd(B, C, x, n, sq, ssps, sd, rstd, r_x[ci % 2], r_sq, r_ps, r_sd, r_rstd)
            emit_norm_mod(B, x, n, rstd, md[:, 1, :, s], md[:, 0, :, s], hh, tmp, r_x[ci % 2], r_rstd, md_r, r_tmp,
                          r_h[ci % 2])
            B.dma(hv[:, :, t0:t0 + n], hh[:, :, :n], reads=[r_h[ci % 2]], writes=[io["hT_out_r"]])
        if not defer:
            B.barrier("sp", [io["hT_out_r"]])
            B.flush()


def out_sharded(B, dst, tile, c0, n, reads, wres):
    if c0 < 256:
        for j in range(4):
            B.dma(dst[j, :, 2048:2112], tile[:, 64 * j:64 * j + 64], reads=reads, writes=[wres])
    else:
        lt = c0 - 256
        j, off = lt // 2048, lt % 2048
        B.dma(dst[j, :, off:off + n], tile[:, :n], reads=reads, writes=[wres])


def canon_chunks():
    ch = [(0, 256, "ctx")]
    for c in range(16):
        ch.append((256 + 512 * c, 512, "lat"))
    return ch


def phase_F(B, nc, es, C, io, dbg=None, after_attn=None):
    with ExitStack() as s_out:
        uT = sb(nc, s_out, "f_uT", [128, NTOK], BF16)
        uTr = sb(nc, s_out, "f_uTr", [128, NTOK], BF16)
        r_uT, r_uTr = Res(), Res()
        with ExitStack() as s_mid:
            qT = sb(nc, s_mid, "f_qT", [128, 2, NTOK], BF16)
            kT = sb(nc, s_mid, "f_kT", [128, NTOK], BF16)
            vv = sb(nc, s_mid, "f_v", [128, NTOK // 128, 128], BF16)
            r_q, r_k, r_v = Res(), Res(), Res()
            f_proj(B, nc, C, io, qT, kT, vv, uT, uTr, r_q, r_k, r_v, r_uT, r_uTr, dbg)
            f_attn(B, nc, C, io, qT, kT, vv, r_q, r_k, r_v, dbg)
            if after_attn is not None:
                after_attn()
        f_s5(B, nc, C, io, uT, uTr, r_uT, r_uTr, dbg)


def f_proj(B, nc, C, io, qT, kT, vv, uT, uTr, r_q, r_k, r_v, r_uT, r_uTr, dbg):
    with ExitStack() as ls:
        wst = sb(nc, ls, "fp_wst", [128, 8, 768], F32)
        w6 = sb(nc, ls, "fp_w6", [128, 8, 768], BF16)
        hb = [sb(nc, ls, "fp_h%d" % i, [128, 8, 512], BF16) for i in range(2)]
        ropec = sb(nc, ls, "fp_ropec", [128, 2], F32)
        rowi = sb(nc, ls, "fp_rowi", [128, 512], F32)
        coli = sb(nc, ls, "fp_coli", [128, 512], F32)
        base = sb(nc, ls, "fp_base", [128, 512], F32)
        sa8 = sb(nc, ls, "fp_sa8", [128, 16], F32)
        ang = sb(nc, ls, "fp_ang", [128, 512], F32)
        angr = sb(nc, ls, "fp_angr", [128, 512], F32)
        tmpr = sb(nc, ls, "fp_tmpr", [128, 512], F32)
        cosT = sb(nc, ls, "fp_cos", [128, 512], F32)
        sinT = sb(nc, ls, "fp_sin", [128, 512], F32)
        t1 = sb(nc, ls, "fp_t1", [128, 512], F32)
        t2 = sb(nc, ls, "fp_t2", [128, 512], F32)
        pbank = [ps(nc, ls, "fp_p%d" % i, [128, 512]) for i in range(5)]
        vbank = ps(nc, ls, "fp_pv", [128, 4, 128])
        r_wst, r_w6, r_rc, r_base, r_ang, r_cs, r_t = Res(), Res(), Res(), Res(), Res(), Res(), Res()
        r_h = [Res(), Res()]
        r_pb = [Res() for _ in range(5)]
        r_vb = Res()
        B.op("pool", lambda h: h.memset(qT[:], 0.0), writes=[r_q])
        B.dma(wst[:], io["w6"].rearrange("(k p) n -> p k n", p=128), writes=[r_wst])
        B.op("act", lambda h: h.activation(w6[:, 0:4, :], wst[:, 0:4, :], AF.Copy), reads=[r_wst], writes=[r_w6])
        B.op("dve", lambda h: h.tensor_copy(w6[:, 4:8, :], wst[:, 4:8, :]), reads=[r_wst], writes=[r_w6])
        B.dma(ropec[:], io["ropec"], writes=[r_rc])
        B.op("pool", lambda h: h.iota(rowi[:], pattern=[[1, 8], [0, 64]], base=0, channel_multiplier=0,
                                      allow_small_or_imprecise_dtypes=True), writes=[r_base])
        B.op("pool", lambda h: h.iota(coli[:], pattern=[[0, 8], [1, 64]], base=0, channel_multiplier=0,
                                      allow_small_or_imprecise_dtypes=True), writes=[r_base])
        B.op("pool", lambda h: h.iota(sa8[:], pattern=[[8, 16]], base=0, channel_multiplier=0,
                                      allow_small_or_imprecise_dtypes=True), writes=[r_base])
        B.op("dve", lambda h: h.tensor_scalar(base[:], rowi[:], ropec[:, 0:1], None, ALU.mult),
             reads=[r_base, r_rc], writes=[r_base])
        B.op("dve", lambda h: h.scalar_tensor_tensor(base[:], coli[:], ropec[:, 1:2], base[:], ALU.mult, ALU.add),
             reads=[r_base, r_rc], writes=[r_base])
        B.op("dve", lambda h: h.tensor_scalar(sa8[:], sa8[:], ropec[:, 0:1], None, ALU.mult),
             reads=[r_base, r_rc], writes=[r_base])
        if "hT_pk" in io:
            hpk = io["hT_pk"]
        else:
            hpk = [io["hT_all"][r].rearrange("(k p) n -> p k n", p=128) for r in range(4)]
        for ci, (c0, n, kind) in enumerate(canon_chunks()):
            hh = hb[ci % 2]
            rh = r_h[ci % 2]
            if kind == "ctx":
                for r in range(4):
                    B.dma(hh[:, :, 64 * r:64 * r + 64], hpk[r][:, :, 2048:2112], writes=[rh],
                          reads=[io["hT_all_r"]])
            else:
                lt = c0 - 256
                r, off = lt // 2048, lt % 2048
                B.dma(hh[:, :, :n], hpk[r][:, :, off:off + n], writes=[rh], reads=[io["hT_all_r"]])
            for oi, wc in enumerate((0, 1, 2, 3, 5)):
                def mm(h, oi=oi, wc=wc, hh=hh, n=n):
                    ins = None
                    for k in range(8):
                        ins = h.matmul(pbank[oi][:, :n], w6[:, k, wc * 128:(wc + 1) * 128], hh[:, k, :n],
                                       start=(k == 0), stop=(k == 7))
                    return ins
                B.op("pe", mm, reads=[rh, r_w6], writes=[r_pb[oi]])

            def mmv(h, hh=hh, n=n):
                ins = None
                for st in range(n // 128):
                    for k in range(8):
                        ins = h.matmul(vbank[:, st, :], hh[:, k, st * 128:(st + 1) * 128], w6[:, k, 512:640],
                                       start=(k == 0), stop=(k == 7))
                return ins
            B.op("pe", mmv, reads=[rh, r_w6], writes=[r_vb])
            t_lo = c0 // 128
            B.op("act", lambda h, n=n, t_lo=t_lo: h.activation(vv[:, t_lo:t_lo + n // 128, :], vbank[:, 0:n // 128, :],
                                                                AF.Copy), reads=[r_vb], writes=[r_v])
            B.op("act", lambda h, c0=c0, n=n: h.activation(uT[:, c0:c0 + n], pbank[4][:, :n], AF.Copy),
                 reads=[r_pb[4]], writes=[r_uT])
            if kind == "ctx":
                lo = 0
            else:
                lo = 256 + 8192 - (c0 - 256) - 512
            B.op("dve", lambda h, lo=lo, n=n: h.tensor_copy(uTr[:, lo:lo + n][:, ::-1], pbank[4][:, :n]),
                 reads=[r_pb[4]], writes=[r_uTr])
            if kind == "ctx":
                B.op("act", lambda h, n=n: h.activation(qT[0:64, 0, 0:n], pbank[0][0:64, :n], AF.Copy),
                     reads=[r_pb[0], r_q], writes=[r_q])
                B.op("act", lambda h, n=n: h.activation(qT[64:128, 1, 0:n], pbank[0][64:128, :n], AF.Copy),
                     reads=[r_pb[0], r_q], writes=[r_q])
                B.op("act", lambda h, n=n: h.activation(kT[:, 0:n], pbank[2][:, :n], AF.Copy),
                     reads=[r_pb[2]], writes=[r_k])
            else:
                c = (c0 - 256) // 512
                B.op("dve", lambda h, c=c: h.tensor_scalar(ang[:], base[:], sa8[:, c:c + 1], None, ALU.add),
                     reads=[r_base], writes=[r_ang])
                range_reduce(B, "dve", angr[:], ang[:], tmpr[:], [r_ang], [r_cs])
                B.op("act", lambda h: h.activation(sinT[:], angr[:], AF.Sin), reads=[r_cs], writes=[r_cs])
                range_reduce(B, "dve", angr[:], ang[:], tmpr[:], [r_ang], [r_cs], add_half_pi=True)
                B.op("act", lambda h: h.activation(cosT[:], angr[:], AF.Sin), reads=[r_cs], writes=[r_cs])
                for (dst, r_dst, pa, pb_) in ((qT, r_q, 0, 1), (kT, r_k, 2, 3)):
                    B.op("dve", lambda h, pa=pa: h.tensor_tensor(t1[:], pbank[pa][:], cosT[:], ALU.mult),
                         reads=[r_pb[pa], r_cs], writes=[r_t])
                    B.op("dve", lambda h, pb_=pb_: h.tensor_tensor(t2[:], pbank[pb_][:], sinT[:], ALU.mult),
                         reads=[r_pb[pb_], r_cs], writes=[r_t])
                    if dst is qT:
                        B.op("dve", lambda h, c0=c0: h.tensor_tensor(qT[0:64, 0, c0:c0 + 512], t1[0:64, :], t2[0:64, :], ALU.add),
                             reads=[r_t, r_q], writes=[r_q])
                        B.op("dve", lambda h, c0=c0: h.tensor_tensor(qT[64:128, 1, c0:c0 + 512], t1[64:128, :], t2[64:128, :], ALU.add),
                             reads=[r_t, r_q], writes=[r_q])
                    else:
                        B.op("dve", lambda h, dst=dst, c0=c0: h.tensor_tensor(dst[:, c0:c0 + 512], t1[:], t2[:], ALU.add),
                             reads=[r_t], writes=[r_dst])
        if dbg is not None:
            B.dma(dbg["qT"], qT[:, 0, :], reads=[r_q], writes=[dbg["r"]])
            B.dma(dbg["kT"], kT[:], reads=[r_k], writes=[dbg["r"]])
            B.dma(dbg["v"], vv[:], reads=[r_v], writes=[dbg["r"]])
            B.dma(dbg["uT"], uT[:], reads=[r_uT], writes=[dbg["r"]])
            B.dma(dbg["uTr"], uTr[:], reads=[r_uTr], writes=[dbg["r"]])
            B.barrier("sp", [dbg["r"]])
        B.flush()


def f_attn(B, nc, C, io, qT, kT, vv, r_q, r_k, r_v, dbg):
    with ExitStack() as ls:
        NE = 4
        ET = [sb(nc, ls, "fa_e%d" % i, [128, 512], BF16) for i in range(NE)]
        r_e = [Res() for _ in range(NE)]
        lamrow = sb(nc, ls, "fa_lamrow", [1, 256], F32)
        lamp = sb(nc, ls, "fa_lamp", [1, 128], F32)
        lsum = sb(nc, ls, "fa_lsum", [1, 4], F32)
        lcon = sb(nc, ls, "fa_lcon", [1, 4], F32)
        ones1 = sb(nc, ls, "fa_ones1", [1, 128], F32)
        lamv = sb(nc, ls, "fa_lamv", [1, 2], F32)
        lamb = sb(nc, ls, "fa_lamb", [128, 2], F32)
        sg = sb(nc, ls, "fa_sg", [128, 1], F32)
        r0 = sb(nc, ls, "fa_r0", [128, 512], F32)
        r1 = sb(nc, ls, "fa_r1", [128, 512], F32)
        t0 = sb(nc, ls, "fa_t0", [128, 512], F32)
        t1 = sb(nc, ls, "fa_t1", [128, 512], F32)
        oo = sb(nc, ls, "fa_o", [128, 512], F32)
        sq = sb(nc, ls, "fa_sq", [128, 512], BF16)
        sd = sb(nc, ls, "fa_sd", [128, 512], F32)
        rs = sb(nc, ls, "fa_rs", [128, 512], F32)
        onb = [sb(nc, ls, "fa_on%d" % i, [128, 512], BF16) for i in range(2)]
        NS = 3
        Sb = [ps(nc, ls, "fa_S%d" % i, [128, 512]) for i in range(NS)]
        Eacc = [sb(nc, ls, "fa_eacc%d" % i, [128, 512], F32) for i in range(2)]
        EaccP = [sb(nc, ls, "fa_eaccp%d" % i, [128, 512], F32) for i in range(2)]
        r_eap = [Res(), Res()]
        r_ea = [Res(), Res()]
        r_sa = [Res(), Res()]
        oacc = [ps(nc, ls, "fa_oa%d" % i, [128, 512]) for i in range(2)]
        sacc = [ps(nc, ls, "fa_sa%d" % i, [128, 512]) for i in range(2)]
        msb = ps(nc, ls, "fa_ms", [128, 512])
        lps = msb[:, 0:2]
        r_S = [Res() for _ in range(NS)]
        r_oa = [Res(), Res()]
        r_ms, r_lam, r_ep, r_on = Res(), Res(), Res(), [Res(), Res()]
        B.dma(lamrow[:], io["lam_qk"].rearrange("(o r) d -> o (r d)", o=1), writes=[r_lam])
        B.dma(lcon[:], io["lcon"], writes=[r_lam])
        B.dma(sg[:], io["subln_g"].rearrange("(p o) -> p o", o=1), writes=[r_lam], slow=True)
        B.op("pool", lambda h: h.memset(ones1[:], 1.0), writes=[r_lam])
        B.op("dve", lambda h: h.tensor_tensor(lamp[:, 0:64], lamrow[:, 0:64], lamrow[:, 64:128], ALU.mult),
             reads=[r_lam], writes=[r_lam])
        B.op("dve", lambda h: h.tensor_tensor(lamp[:, 64:128], lamrow[:, 128:192], lamrow[:, 192:256], ALU.mult),
             reads=[r_lam], writes=[r_lam])
        B.op("dve", lambda h: h.reduce_sum(lsum[:, 0:1], lamp[:, 0:64], axis=mybir.AxisListType.X),
             reads=[r_lam], writes=[r_lam])
        B.op("dve", lambda h: h.reduce_sum(lsum[:, 1:2], lamp[:, 64:128], axis=mybir.AxisListType.X),
             reads=[r_lam], writes=[r_lam])
        B.op("act", lambda h: h.activation(lsum[:, 2:4], lsum[:, 0:2], AF.Exp), reads=[r_lam], writes=[r_lam])
        B.op("dve", lambda h: h.tensor_tensor(lamv[:, 0:1], lsum[:, 3:4], lsum[:, 2:3], ALU.subtract),
             reads=[r_lam], writes=[r_lam])
        B.op("dve", lambda h: h.tensor_tensor(lamv[:, 0:1], lamv[:, 0:1], lcon[:, 0:1], ALU.subtract),
             reads=[r_lam], writes=[r_lam])
        B.op("dve", lambda h: h.tensor_copy(lamv[:, 1:2], lcon[:, 1:2]), reads=[r_lam], writes=[r_lam])
        B.op("pe", lambda h: h.matmul(lps, ones1[:], lamv[:], start=True, stop=True), reads=[r_lam], writes=[r_ms])
        B.op("dve", lambda h: h.tensor_copy(lamb[:], lps), reads=[r_ms], writes=[r_lam])
        B.op("dve", lambda h: h.tensor_tensor(sg[:], sg[:], lamb[:, 1:2], ALU.mult), reads=[r_lam], writes=[r_lam])
        dump(B, "lamb", lamb[:], [128, 2], F32, [r_lam])
        dump(B, "sg", sg[:], [128, 1], F32, [r_lam])
        dump(B, "lsum", lsum[:], [1, 4], F32, [r_lam])
        dump(B, "lamv", lamv[:], [1, 2], F32, [r_lam])
        qblocks = [(0, 256, 2)]
        for c in range(16):
            qblocks.append((256 + 512 * c, 512, NTOK // 128))
        if DBG["skip_attn"]:
            qblocks = qblocks[:2]
        onv = io["onT_out"]
        gi = 0
        for bi, (q0, nq, nkt) in enumerate(qblocks):
            its = [(kt, c) for kt in range(nkt) for c in range(2)]

            def emit_qk(i, q0=q0, nq=nq):
                kt, c = its[i]
                S, rS = Sb[(gi + i) % NS], r_S[(gi + i) % NS]
                B.op("pe", lambda h: h.matmul(
                    S[:, :nq], kT[:, kt * 128:(kt + 1) * 128], qT[:, c, q0:q0 + nq],
                    start=True, stop=True), reads=[r_q, r_k], writes=[rS])
            LOOK = 2
            for i in range(min(LOOK, len(its))):
                emit_qk(i)
            for i, (kt, c) in enumerate(its):
                S, rS = Sb[(gi + i) % NS], r_S[(gi + i) % NS]
                E, rE = ET[(gi + i) % NE], r_e[(gi + i) % NE]
                B.op("act", lambda h, S=S, E=E, nq=nq: h.activation(E[:, :nq], S[:, :nq], AF.Exp, scale=0.125),
                     reads=[rS], writes=[rE])
                if i + LOOK < len(its):
                    emit_qk(i + LOOK)
                B.op("pe", lambda h, E=E, c=c, kt=kt, nq=nq, nkt=nkt: h.matmul(
                    oacc[c][:, :nq], vv[:, kt, :], E[:, :nq], start=(kt == 0), stop=(kt == nkt - 1)),
                    reads=[rE, r_v], writes=[r_oa[c]])
                if kt % 3 == 2:
                    aeng, acc, racc = "pool", EaccP[c], r_eap[c]
                    first = (kt == 2)
                else:
                    aeng, acc, racc = "dve", Eacc[c], r_ea[c]
                    first = (kt == 0)
                if first:
                    B.op(aeng, lambda h, E=E, acc=acc, nq=nq: h.tensor_copy(acc[:, :nq], E[:, :nq]),
                         reads=[rE], writes=[racc])
                else:
                    B.op(aeng, lambda h, E=E, acc=acc, nq=nq: h.tensor_tensor(acc[:, :nq], acc[:, :nq], E[:, :nq], ALU.add),
                         reads=[rE, racc], writes=[racc])
            gi += len(its)
            for c in range(2):
                def sm(h, c=c, nq=nq, nkt=nkt):
                    if nkt > 2:
                        h.matmul(sacc[c][:, :nq], C.ones_f[:], Eacc[c][:, :nq], start=True, stop=False)
                        return h.matmul(sacc[c][:, :nq], C.ones_f[:], EaccP[c][:, :nq], start=False, stop=True)
                    return h.matmul(sacc[c][:, :nq], C.ones_f[:], Eacc[c][:, :nq], start=True, stop=True)
                B.op("pe", sm, reads=[r_ea[c], r_eap[c], C.r], writes=[r_sa[c]])
            n = nq
            B.op("dve", lambda h, n=n: h.reciprocal(r0[:, :n], sacc[0][:, :n]), reads=[r_sa[0]], writes=[r_ep])
            B.op("dve", lambda h, n=n: h.reciprocal(r1[:, :n], sacc[1][:, :n]), reads=[r_sa[1]], writes=[r_ep])
            B.op("dve", lambda h, n=n: h.tensor_tensor(t0[:, :n], oacc[0][:, :n], r0[:, :n], ALU.mult),
                 reads=[r_oa[0], r_ep], writes=[r_ep])
            B.op("dve", lambda h, n=n: h.tensor_tensor(t1[:, :n], oacc[1][:, :n], r1[:, :n], ALU.mult),
                 reads=[r_oa[1], r_ep], writes=[r_ep])
            B.op("dve", lambda h, n=n: h.scalar_tensor_tensor(oo[:, :n], t1[:, :n], lamb[:, 0:1], t0[:, :n],
                                                               ALU.mult, ALU.add), reads=[r_ep, r_lam], writes=[r_ep])
            B.op("act", lambda h, n=n: h.activation(sq[:, :n], oo[:, :n], AF.Square), reads=[r_ep], writes=[r_ep])
            B.op("pe", lambda h, n=n: h.matmul(msb[:, :n], C.ones_bf[:], sq[:, :n], start=True, stop=True),
                 reads=[r_ep, C.r], writes=[r_ms])
            B.op("act", lambda h, n=n: h.activation(sd[:, :n], msb[:, :n], AF.Sqrt, bias=C.eps[:], scale=1.0 / 128),
                 reads=[r_ms, C.r], writes=[r_ep])
            B.op("dve", lambda h, n=n: h.reciprocal(rs[:, :n], sd[:, :n]), reads=[r_ep], writes=[r_ep])
            ob = onb[bi % 2]
            B.op("dve", lambda h, n=n, ob=ob: h.scalar_tensor_tensor(ob[:, :n], oo[:, :n], sg[:, 0:1], rs[:, :n],
                                                                      ALU.mult, ALU.mult),
                 reads=[r_ep, r_lam], writes=[r_on[bi % 2]])
            out_sharded(B, onv, ob, q0, n, [r_on[bi % 2]], io["onT_out_r"])
        B.barrier("sp", [io["onT_out_r"]])
        B.flush()


def f_s5(B, nc, C, io, uT, uTr, r_uT, r_uTr, dbg):
    with ExitStack() as ls:
        yacc = sb(nc, ls, "s5_yacc", [128, NTOK], F32)
        r_ya = Res()
        are = sb(nc, ls, "s5_are", [128, 8], F32)
        aim = sb(nc, ls, "s5_aim", [128, 8], F32)
        ldt = sb(nc, ls, "s5_ldt", [128, 8], F32)
        dtt = sb(nc, ls, "s5_dt", [128, 8], F32)
        th = sb(nc, ls, "s5_th", [128, 8], F32)
        rho = sb(nc, ls, "s5_rho", [128, 8], F32)
        pt = [sb(nc, ls, "s5_pt%d" % i, [128, 8], F32) for i in range(10)]
        cn = sb(nc, ls, "s5_cn", [128, 2, 8], F32)
        sn = sb(nc, ls, "s5_sn", [128, 2, 8], F32)
        nsn = sb(nc, ls, "s5_nsn", [128, 2, 8], F32)
        kre = sb(nc, ls, "s5_kre", [128, 8], F32)
        kim = sb(nc, ls, "s5_kim", [128, 8], F32)
        nkim = sb(nc, ls, "s5_nkim", [128, 8], F32)
        Bre = sb(nc, ls, "s5_Bre", [128, 8, 16], F32)
        Bim = sb(nc, ls, "s5_Bim", [128, 8, 16], F32)
        bbr = sb(nc, ls, "s5_bbr", [128, 8, 16], F32)
        bbi = sb(nc, ls, "s5_bbi", [128, 8, 16], F32)
        Z = sb(nc, ls, "s5_Z", [128, 16, 128], F32)
        BT = sb(nc, ls, "s5_BT", [128, 16, 128], BF16)
        Cst = sb(nc, ls, "s5_Cst", [128, 16, 128], F32)
        CM = sb(nc, ls, "s5_CM", [128, 24, 128], BF16)
        dcol = sb(nc, ls, "s5_dcol", [128, 1], F32)
        Cb = sb(nc, ls, "s5_Cb", [128, 8, 512], F32)
        Sbt = sb(nc, ls, "s5_Sb", [128, 8, 512], F32)
        Rt = sb(nc, ls, "s5_Rt", [128, 8, 512], F32)
        tA = sb(nc, ls, "s5_tA", [128, 512], F32)
        tB = sb(nc, ls, "s5_tB", [128, 512], F32)
        zi = sb(nc, ls, "s5_zi", [128, 8, 2], F32)
        zt = sb(nc, ls, "s5_zt", [128, 2], F32)
        tps = ps(nc, ls, "s5_tps", [128, 128])
        Pb = [ps(nc, ls, "s5_P%d" % i, [128, 512]) for i in range(4)]
        yb = [ps(nc, ls, "s5_y%d" % i, [128, 512]) for i in range(2)]
        r_p, r_B, r_Z, r_BT, r_C, r_CM, r_tab, r_tt, r_zi, r_tps = (Res() for _ in range(10))
        sp = io["s5p"]
        B.dma(are[:].rearrange("q (d r) -> q d r", d=2), sp["a_re"].rearrange("d (r g) p -> (g p) d r", g=2),
              writes=[r_p], slow=True)
        B.dma(aim[:].rearrange("q (d r) -> q d r", d=2), sp["a_im"].rearrange("d (r g) p -> (g p) d r", g=2),
              writes=[r_p], slow=True)
        ldv = sp["log_dt"].rearrange("d (r g) -> g d r", g=2)
        for g2 in range(2):
            B.dma(ldt[64 * g2:64 * g2 + 64, :].rearrange("q (d r) -> q d r", d=2),
                  ldv[g2:g2 + 1].to_broadcast([64, 2, 4]), writes=[r_p], slow=True)
        B.dma(Bre[:].rearrange("q (d r) h -> q d r h", d=2), sp["b_re"].rearrange("d (r g) p h -> (g p) d r h", g=2),
              writes=[r_B], slow=True)
        B.dma(Bim[:].rearrange("q (d r) h -> q d r h", d=2), sp["b_im"].rearrange("d (r g) p h -> (g p) d r h", g=2),
              writes=[r_B], slow=True)
        B.dma(dcol[:], sp["d"].rearrange("(p o) -> p o", o=1), writes=[r_p], slow=True)
        B.op("pool", lambda h: h.memset(Z[:], 0.0), writes=[r_Z])
        B.op("pool", lambda h: h.memset(Cst[:], 0.0), writes=[r_C])
        for d in range(2):
            for pr in range(4):
                j = d * 4 + pr
                for g2 in range(2):
                    g = 2 * pr + g2
                    for ri, nm in enumerate(("c_re", "c_im")):
                        B.dma(Cst[64 * g2:64 * g2 + 64, 2 * j + ri, 16 * g:16 * g + 16],
                              sp[nm][d, g].rearrange("h p -> p h"), reads=[r_C], writes=[r_C], slow=True)
        V = lambda t: t[:]
        B.op("act", lambda h: h.activation(dtt[:], ldt[:], AF.Exp), reads=[r_p], writes=[r_p])
        B.op("dve", lambda h: h.tensor_tensor(th[:], aim[:], dtt[:], ALU.mult), reads=[r_p], writes=[r_p])
        B.op("dve", lambda h: h.tensor_tensor(pt[0][:], are[:], dtt[:], ALU.mult), reads=[r_p], writes=[r_p])
        B.op("act", lambda h: h.activation(rho[:], pt[0][:], AF.Exp), reads=[r_p], writes=[r_p])
        range_reduce(B, "dve", pt[3][:], th[:], pt[4][:], [r_p], [r_p])
        B.op("dve", lambda h: h.tensor_copy(th[:], pt[3][:]), reads=[r_p], writes=[r_p])
        range_reduce(B, "dve", pt[3][:], th[:], pt[4][:], [r_p], [r_p])
        B.op("act", lambda h: h.activation(pt[1][:], pt[3][:], AF.Sin), reads=[r_p], writes=[r_p])
        range_reduce(B, "dve", pt[3][:], th[:], pt[4][:], [r_p], [r_p], add_half_pi=True)
        B.op("act", lambda h: h.activation(pt[2][:], pt[3][:], AF.Sin), reads=[r_p], writes=[r_p])
        B.op("dve", lambda h: h.tensor_tensor(pt[5][:], rho[:], pt[2][:], ALU.mult), reads=[r_p], writes=[r_p])
        B.op("dve", lambda h: h.tensor_scalar(pt[5][:], pt[5][:], -1.0, None, ALU.add), reads=[r_p], writes=[r_p])
        B.op("dve", lambda h: h.tensor_tensor(pt[6][:], rho[:], pt[1][:], ALU.mult), reads=[r_p], writes=[r_p])
        B.op("dve", lambda h: h.tensor_tensor(pt[7][:], are[:], are[:], ALU.mult), reads=[r_p], writes=[r_p])
        B.op("dve", lambda h: h.tensor_tensor(pt[8][:], aim[:], aim[:], ALU.mult), reads=[r_p], writes=[r_p])
        B.op("dve", lambda h: h.tensor_tensor(pt[7][:], pt[7][:], pt[8][:], ALU.add), reads=[r_p], writes=[r_p])
        B.op("dve", lambda h: h.reciprocal(pt[7][:], pt[7][:]), reads=[r_p], writes=[r_p])
        B.op("dve", lambda h: h.tensor_tensor(pt[8][:], pt[5][:], are[:], ALU.mult), reads=[r_p], writes=[r_p])
        B.op("dve", lambda h: h.tensor_tensor(pt[9][:], pt[6][:], aim[:], ALU.mult), reads=[r_p], writes=[r_p])
        B.op("dve", lambda h: h.tensor_tensor(pt[8][:], pt[8][:], pt[9][:], ALU.add), reads=[r_p], writes=[r_p])
        B.op("dve", lambda h: h.tensor_tensor(kre[:], pt[8][:], pt[7][:], ALU.mult), reads=[r_p], writes=[r_p])
        B.op("dve", lambda h: h.tensor_tensor(pt[8][:], pt[6][:], are[:], ALU.mult), reads=[r_p], writes=[r_p])
        B.op("dve", lambda h: h.tensor_tensor(pt[9][:], pt[5][:], aim[:], ALU.mult), reads=[r_p], writes=[r_p])
        B.op("dve", lambda h: h.tensor_tensor(pt[8][:], pt[8][:], pt[9][:], ALU.subtract), reads=[r_p], writes=[r_p])
        B.op("dve", lambda h: h.tensor_tensor(kim[:], pt[8][:], pt[7][:], ALU.mult), reads=[r_p], writes=[r_p])
        B.op("dve", lambda h: h.tensor_scalar(nkim[:], kim[:], -1.0, None, ALU.mult), reads=[r_p], writes=[r_p])
        for ni, nn in enumerate((512.0, 256.0)):
            B.op("dve", lambda h, nn=nn: h.tensor_scalar(pt[0][:], th[:], nn, None, ALU.mult), reads=[r_p], writes=[r_p])
            range_reduce(B, "dve", pt[3][:], pt[0][:], pt[4][:], [r_p], [r_p])
            B.op("act", lambda h, ni=ni: h.activation(sn[:, ni, :], pt[3][:], AF.Sin), reads=[r_p], writes=[r_p])
            range_reduce(B, "dve", pt[3][:], pt[0][:], pt[4][:], [r_p], [r_p], add_half_pi=True)
            B.op("act", lambda h, ni=ni: h.activation(cn[:, ni, :], pt[3][:], AF.Sin), reads=[r_p], writes=[r_p])
        B.op("dve", lambda h: h.tensor_scalar(nsn[:], sn[:], -1.0, None, ALU.mult), reads=[r_p], writes=[r_p])
        for j in range(8):
            B.op("dve", lambda h, j=j: h.tensor_scalar(bbr[:, j, :], Bre[:, j, :], kre[:, j:j + 1], None, ALU.mult),
                 reads=[r_p, r_B], writes=[r_B])
            B.op("dve", lambda h, j=j: h.scalar_tensor_tensor(bbr[:, j, :], Bim[:, j, :], nkim[:, j:j + 1], bbr[:, j, :],
                                                               ALU.mult, ALU.add), reads=[r_p, r_B], writes=[r_B])
            B.op("dve", lambda h, j=j: h.tensor_scalar(bbi[:, j, :], Bim[:, j, :], kre[:, j:j + 1], None, ALU.mult),
                 reads=[r_p, r_B], writes=[r_B])
            B.op("dve", lambda h, j=j: h.scalar_tensor_tensor(bbi[:, j, :], Bre[:, j, :], kim[:, j:j + 1], bbi[:, j, :],
                                                               ALU.mult, ALU.add), reads=[r_p, r_B], writes=[r_B])
        for j in range(8):
            pr = j % 4
            for g2 in range(2):
                g = 2 * pr + g2
                for ri, src in enumerate((bbr, bbi)):
                    B.op("dve", lambda h, j=j, g2=g2, g=g, ri=ri, src=src: h.tensor_copy(
                        Z[64 * g2:64 * g2 + 64, 2 * j + ri, 16 * g:16 * g + 16], src[64 * g2:64 * g2 + 64, j, :]),
                        reads=[r_B, r_Z], writes=[r_Z])
        for m in range(16):
            B.op("pe", lambda h, m=m: h.matmul(tps[:], Z[:, m, :], C.ident[:], start=True, stop=True),
                 reads=[r_Z, C.r], writes=[r_tps])
            B.op("act", lambda h, m=m: h.activation(BT[:, m, :], tps[:], AF.Copy), reads=[r_tps], writes=[r_BT])
        for j in range(8):
            B.op("act", lambda h, j=j: h.activation(CM[:, 3 * j, :], Cst[:, 2 * j, :], AF.Copy), reads=[r_C], writes=[r_CM])
            B.op("dve", lambda h, j=j: h.tensor_scalar(CM[:, 3 * j + 1, :], Cst[:, 2 * j, :], -1.0, None, ALU.mult),
                 reads=[r_C], writes=[r_CM])
            B.op("dve", lambda h, j=j: h.tensor_scalar(CM[:, 3 * j + 2, :], Cst[:, 2 * j + 1, :], -1.0, None, ALU.mult),
                 reads=[r_C], writes=[r_CM])
        for j in range(8):
            B.op("dve", lambda h, j=j: h.tensor_scalar(tA[:], C.iota[:], th[:, j:j + 1], None, ALU.mult),
                 reads=[r_p, C.r], writes=[r_tt])
            range_reduce(B, "dve", tB[:], tA[:], Rt[:, j, :], [r_tt], [r_tab])
            B.op("act", lambda h, j=j: h.activation(Sbt[:, j, :], tB[:], AF.Sin), reads=[r_tab], writes=[r_tab])
            range_reduce(B, "dve", tB[:], tA[:], Rt[:, j, :], [r_tt], [r_tab], add_half_pi=True)
            B.op("act", lambda h, j=j: h.activation(Cb[:, j, :], tB[:], AF.Sin), reads=[r_tab], writes=[r_tab])
            B.op("dve", lambda h, j=j: h.tensor_scalar(Rt[:, j, :], C.iota[:], 0.0, rho[:, j:j + 1], ALU.mult, ALU.add),
                 reads=[r_tab, r_p, C.r], writes=[r_tab])
        B.op("pool", lambda h: h.memset(zi[:], 0.0), writes=[r_zi])
        for nm, t in (("th", th), ("rho", rho), ("kre", kre), ("kim", kim), ("are", are), ("aim", aim), ("dtt", dtt)):
            dump(B, nm, t[:], [128, 8], F32, [r_p])
        dump(B, "cn", cn[:], [128, 2, 8], F32, [r_p])
        dump(B, "sn", sn[:], [128, 2, 8], F32, [r_p])
        dump(B, "BT", BT[:], [128, 16, 128], BF16, [r_BT])
        dump(B, "CM", CM[:], [128, 24, 128], BF16, [r_CM])
        dump(B, "Cb0", Cb[:, 0, :], [128, 512], F32, [r_tab])
        dump(B, "Sb0", Sbt[:, 0, :], [128, 512], F32, [r_tab])
        dump(B, "Rt0", Rt[:, 0, :], [128, 512], F32, [r_tab])
        B.flush()
        NW = 2
        w_re = [sb(nc, ls, "s5_wre%d" % i, [128, 512], F32) for i in range(NW)]
        w_im = [sb(nc, ls, "s5_wim%d" % i, [128, 512], F32) for i in range(NW)]
        z_re = [sb(nc, ls, "s5_zre%d" % i, [128, 512], F32) for i in range(NW)]
        z_im = [sb(nc, ls, "s5_zim%d" % i, [128, 512], F32) for i in range(NW)]
        m1 = sb(nc, ls, "s5_m1", [128, 512], F32)
        m2 = sb(nc, ls, "s5_m2", [128, 512], F32)
        Vt = [sb(nc, ls, "s5_V%d" % i, [128, 4, 512], BF16) for i in range(NW)]
        r_w = [Res() for _ in range(NW)]
        r_z = [Res() for _ in range(NW)]
        r_V = [Res() for _ in range(NW)]
        r_m = Res()
        r_P = [Res() for _ in range(4)]
        r_y = [Res(), Res()]
        it = 0
        yi = 0
        chunks = canon_chunks()
        m1c = [m1, sb(nc, ls, "s5_m1b", [128, 512], F32)]
        m2c = [m2, sb(nc, ls, "s5_m2b", [128, 512], F32)]
        ztc = [zt, sb(nc, ls, "s5_ztb", [128, 2], F32)]
        r_mc = [Res(), Res()]
        r_ztc = [Res(), Res()]
        r_zij = [Res() for _ in range(8)]
        for r_ in r_zij:
            r_.w = r_zi.w

        def make_iter(d, c0, n, ni, pr, it, ybk, ry, src, r_src):
            ops = []
            A = lambda *a: ops.append(a)
            j = d * 4 + pr
            wi = it % NW
            ch = it % 2
            Pre, Pim = Pb[2 * ch], Pb[2 * ch + 1]
            rPre, rPim = r_P[2 * ch], r_P[2 * ch + 1]
            wr, wim_, zr, zim_ = w_re[wi], w_im[wi], z_re[wi], z_im[wi]
            mA, mB, zt_, rm, rzt, rzi = m1c[ch], m2c[ch], ztc[ch], r_mc[ch], r_ztc[ch], r_zij[j]
            VV = Vt[wi]
            A("pe", lambda h: h.matmul(Pre[:, :n], BT[:, 2 * j, :], src[:, c0:c0 + n], start=True, stop=True),
              [r_BT, r_src], [rPre])
            A("pe", lambda h: h.matmul(Pim[:, :n], BT[:, 2 * j + 1, :], src[:, c0:c0 + n], start=True, stop=True),
              [r_BT, r_src], [rPim])
            A("dve", lambda h: h.tensor_tensor(mA[:, :n], Pre[:, :n], Cb[:, j, :n], ALU.mult), [rPre, r_tab], [rm])
            A("dve", lambda h: h.tensor_tensor(mB[:, :n], Pim[:, :n], Sbt[:, j, :n], ALU.mult), [rPim, r_tab], [rm])
            A("dve", lambda h: h.tensor_tensor(wr[:, :n], mA[:, :n], mB[:, :n], ALU.add), [rm], [r_w[wi]])
            A("dve", lambda h: h.tensor_tensor(mA[:, :n], Pim[:, :n], Cb[:, j, :n], ALU.mult), [rPim, r_tab], [rm])
            A("dve", lambda h: h.tensor_tensor(mB[:, :n], Pre[:, :n], Sbt[:, j, :n], ALU.mult), [rPre, r_tab], [rm])
            A("dve", lambda h: h.tensor_tensor(wim_[:, :n], mA[:, :n], mB[:, :n], ALU.subtract), [rm], [r_w[wi]])
            A("dve", lambda h: h.tensor_tensor_scan(zr[:, :n], Rt[:, j, :n], wr[:, :n], zi[:, j, 0:1], ALU.mult, ALU.add),
              [r_w[wi], r_tab, rzi], [r_z[wi]])
            A("dve", lambda h: h.tensor_tensor_scan(zim_[:, :n], Rt[:, j, :n], wim_[:, :n], zi[:, j, 1:2], ALU.mult, ALU.add),
              [r_w[wi], r_tab, rzi], [r_z[wi]])
            A("dve", lambda h: h.tensor_scalar(zt_[:, 0:1], zr[:, n - 1:n], cn[:, ni, j:j + 1], None, ALU.mult),
              [r_z[wi], r_p], [rzt])
            A("dve", lambda h: h.tensor_scalar(zt_[:, 1:2], zr[:, n - 1:n], sn[:, ni, j:j + 1], None, ALU.mult),
              [r_z[wi], r_p], [rzt])
            A("dve", lambda h: h.scalar_tensor_tensor(zi[:, j, 0:1], zim_[:, n - 1:n], nsn[:, ni, j:j + 1], zt_[:, 0:1],
                                                      ALU.mult, ALU.add), [r_z[wi], r_p, rzt], [rzi])
            A("dve", lambda h: h.scalar_tensor_tensor(zi[:, j, 1:2], zim_[:, n - 1:n], cn[:, ni, j:j + 1], zt_[:, 1:2],
                                                      ALU.mult, ALU.add), [r_z[wi], r_p, rzt], [rzi])
            A("pool", lambda h: h.tensor_tensor(VV[:, 0, :n], zr[:, :n], Cb[:, j, :n], ALU.mult), [r_z[wi], r_tab], [r_V[wi]])
            A("pool", lambda h: h.tensor_tensor(VV[:, 1, :n], zim_[:, :n], Sbt[:, j, :n], ALU.mult), [r_z[wi], r_tab], [r_V[wi]])
            A("pool", lambda h: h.tensor_tensor(VV[:, 2, :n], zr[:, :n], Sbt[:, j, :n], ALU.mult), [r_z[wi], r_tab], [r_V[wi]])
            A("pool", lambda h: h.tensor_tensor(VV[:, 3, :n], zim_[:, :n], Cb[:, j, :n], ALU.mult), [r_z[wi], r_tab], [r_V[wi]])

            def rd(h):
                h.matmul(ybk[:, :n], CM[:, 3 * j, :], VV[:, 0, :n], start=(pr == 0), stop=False)
                h.matmul(ybk[:, :n], CM[:, 3 * j + 1, :], VV[:, 1, :n], start=False, stop=False)
                h.matmul(ybk[:, :n], CM[:, 3 * j + 2, :], VV[:, 2, :n], start=False, stop=False)
                return h.matmul(ybk[:, :n], CM[:, 3 * j + 2, :], VV[:, 3, :n], start=False, stop=(pr == 3))
            A("pe", rd, [r_V[wi], r_CM], [ry])
            return ops

        for d in range(2):
            src, r_src = (uT, r_uT) if d == 0 else (uTr, r_uTr)
            for ci, (c0, n, kind) in enumerate(chunks):
                ni = 1 if kind == "ctx" else 0
                ybk = yb[yi % 2]
                ry = r_y[yi % 2]
                yi += 1
                for pp in range(2):
                    la = make_iter(d, c0, n, ni, 2 * pp, it, ybk, ry, src, r_src)
                    lb = make_iter(d, c0, n, ni, 2 * pp + 1, it + 1, ybk, ry, src, r_src)
                    it += 2
                    for oa, ob in zip(la, lb):
                        B.op(oa[0], oa[1], reads=oa[2], writes=oa[3])
                        B.op(ob[0], ob[1], reads=ob[2], writes=ob[3])
                if d == 0:
                    B.op("act", lambda h, ybk=ybk, c0=c0, n=n: h.activation(yacc[:, c0:c0 + n], ybk[:, :n], AF.Copy),
                         reads=[ry], writes=[r_ya])
                else:
                    lo = 0 if kind == "ctx" else 256 + 8192 - (c0 - 256) - 512
                    B.op("dve", lambda h, ybk=ybk, lo=lo, n=n: h.tensor_tensor(
                        yacc[:, lo:lo + n][:, ::-1], yacc[:, lo:lo + n][:, ::-1], ybk[:, :n], ALU.add),
                        reads=[ry, r_ya], writes=[r_ya])
            if d == 0:
                B.op("pool", lambda h: h.memset(zi[:], 0.0), reads=r_zij, writes=r_zij)
        g1 = [sb(nc, ls, "s5_g1%d" % i, [128, 512], F32) for i in range(2)]
        g2_ = [sb(nc, ls, "s5_g2%d" % i, [128, 512], F32) for i in range(2)]
        g3 = [sb(nc, ls, "s5_g3%d" % i, [128, 512], F32) for i in range(2)]
        yo = [sb(nc, ls, "s5_yo%d" % i, [128, 512], BF16) for i in range(2)]
        r_g = [Res(), Res()]
        r_yo = [Res(), Res()]
        ygv = io["ygT_out"]
        KG = math.sqrt(2.0 / math.pi)
        for ci, (c0, n, kind) in enumerate(chunks):
            b = ci % 2
            yt, x2, th_ = g1[b], g2_[b], g3[b]
            B.op("dve", lambda h, yt=yt, c0=c0, n=n: h.scalar_tensor_tensor(
                yt[:, :n], uT[:, c0:c0 + n], dcol[:, 0:1], yacc[:, c0:c0 + n], ALU.mult, ALU.add),
                reads=[r_uT, r_ya, r_p], writes=[r_g[b]])
            B.op("pool", lambda h, yt=yt, x2=x2, n=n: h.tensor_tensor(x2[:, :n], yt[:, :n], yt[:, :n], ALU.mult),
                 reads=[r_g[b]], writes=[r_g[b]])
            B.op("pool", lambda h, x2=x2, n=n: h.tensor_scalar(x2[:, :n], x2[:, :n], 0.044715, 1.0, ALU.mult, ALU.add),
                 reads=[r_g[b]], writes=[r_g[b]])
            B.op("pool", lambda h, yt=yt, x2=x2, n=n: h.tensor_tensor(x2[:, :n], x2[:, :n], yt[:, :n], ALU.mult),
                 reads=[r_g[b]], writes=[r_g[b]])
            B.op("act", lambda h, x2=x2, th_=th_, n=n: h.activation(th_[:, :n], x2[:, :n], AF.Tanh, scale=KG),
                 reads=[r_g[b]], writes=[r_g[b]])
            B.op("dve", lambda h, yt=yt, th_=th_, n=n: h.scalar_tensor_tensor(
                th_[:, :n], th_[:, :n], 1.0, yt[:, :n], ALU.add, ALU.mult), reads=[r_g[b]], writes=[r_g[b]])
            B.op("act", lambda h, th_=th_, b=b, n=n: h.activation(yo[b][:, :n], th_[:, :n], AF.Copy, scale=0.5),
                 reads=[r_g[b]], writes=[r_yo[b]])
            out_sharded(B, ygv, yo[b], c0, n, [r_yo[b]], io["ygT_out_r"])
        if dbg is not None:
            B.dma(dbg["yacc"], yacc[:], reads=[r_ya], writes=[dbg["r"]])
            B.barrier("sp", [dbg["r"]])
        B.barrier("sp", [io["ygT_out_r"]])
        B.flush()


def dram_in(nc, name, shape, dt):
    return nc.dram_tensor(name, list(shape), dt, kind="ExternalInput").ap()


def dram_out(nc, name, shape, dt):
    return nc.dram_tensor(name, list(shape), dt, kind="ExternalOutput").ap()


def build_T0():
    nc = bass.Bass("TRN2", target_bir_lowering=False)
    io = {
        "xT": dram_in(nc, "xT", [D, NT], F32),
        "c_b": dram_in(nc, "c_b", [D], F32),
        "c_ctx": dram_in(nc, "c_ctx", [D], F32),
        "ada_w": dram_in(nc, "ada_w", [D, 6 * D], F32),
        "ada_b": dram_in(nc, "ada_b", [6 * D], F32),
        "norm_g": dram_in(nc, "norm_g", [4, D], F32),
        "hT_out": dram_out(nc, "hT_out", [D, NT], BF16),
        "hT_out_r": Res(),
    }
    with ExitStack() as es:
        B = Builder(nc, es)
        C = make_consts(B, nc, es)
        phase_T0(B, nc, es, C, io)
    return nc


def f_io(nc, debug):
    io = {
        "hT_all": dram_in(nc, "hT_all", [4, D, NT], BF16), "hT_all_r": Res(),
        "w6": dram_in(nc, "w6", [D, 768], F32),
        "ropec": dram_in(nc, "ropec", [128, 2], F32),
        "lam_qk": dram_in(nc, "lam_qk", [4, 64], F32),
        "lcon": dram_in(nc, "lcon", [1, 4], F32),
        "subln_g": dram_in(nc, "subln_g", [128], F32),
        "s5p": {
            "a_re": dram_in(nc, "a_re", [2, 8, 64], F32), "a_im": dram_in(nc, "a_im", [2, 8, 64], F32),
            "b_re": dram_in(nc, "b_re", [2, 8, 64, 16], F32), "b_im": dram_in(nc, "b_im", [2, 8, 64, 16], F32),
            "c_re": dram_in(nc, "c_re", [2, 8, 16, 64], F32), "c_im": dram_in(nc, "c_im", [2, 8, 16, 64], F32),
            "log_dt": dram_in(nc, "log_dt", [2, 8], F32), "d": dram_in(nc, "ssm_d", [128], F32),
        },
        "onT_out": dram_out(nc, "onT_out", [4, 128, NT], BF16), "onT_out_r": Res(),
        "ygT_out": dram_out(nc, "ygT_out", [4, 128, NT], BF16), "ygT_out_r": Res(),
    }
    dbg = None
    if debug:
        dbg = {"qT": dram_out(nc, "d_qT", [128, NTOK], BF16), "kT": dram_out(nc, "d_kT", [128, NTOK], BF16),
               "v": dram_out(nc, "d_v", [128, NTOK // 128, 128], BF16), "uT": dram_out(nc, "d_uT", [128, NTOK], BF16),
               "uTr": dram_out(nc, "d_uTr", [128, NTOK], BF16), "yacc": dram_out(nc, "d_yacc", [128, NTOK], F32),
               "r": Res()}
    return io, dbg


def build_F(debug=False):
    nc = bass.Bass("TRN2", target_bir_lowering=False)
    DBG["nc"] = nc
    DBG["on"] = debug
    io, dbg = f_io(nc, debug)
    with ExitStack() as es:
        B = Builder(nc, es)
        C = make_consts(B, nc, es)
        phase_F(B, nc, es, C, io, dbg)
    return nc


def rope_consts():
    sa = np.zeros((128, 2), np.float32)
    for p in range(128):
        d = p % 64
        axis, half, f = d // 32, (d % 32) // 16, d % 16
        inv = np.float32(10000.0 ** (-f / 16.0))
        sgn = -1.0 if half == 0 else 1.0
        sa[p, axis] = sgn * inv
    return sa


def swap_half_cols(w):
    w4 = w.reshape(w.shape[:-1] + (4, 2, 16))
    return np.ascontiguousarray(w4[..., ::-1, :]).reshape(w.shape)


def f_inputs(l, b, f, hT_all_b, inputs):
    w_in = inputs["w_in"][l]
    wq = w_in[:, 128 * f:128 * f + 128]
    wk = w_in[:, 512 + 128 * f:512 + 128 * f + 128]
    wv = w_in[:, 1024 + 128 * f:1024 + 128 * f + 128]
    wu = w_in[:, 1536 + 128 * f:1536 + 128 * f + 128]
    w6 = np.ascontiguousarray(np.concatenate([wq, swap_half_cols(wq), wk, swap_half_cols(wk), wv, wu], axis=1))
    lam_init = 0.8 - 0.6 * math.exp(-0.3 * l)
    gs = slice(8 * f, 8 * f + 8)
    return {
        "hT_all": hT_all_b, "w6": w6, "ropec": rope_consts(),
        "lam_qk": np.ascontiguousarray(inputs["lam_qk"][l]),
        "lcon": np.array([[lam_init, 1.0 - lam_init, 0.0, 0.0]], np.float32),
        "subln_g": np.ascontiguousarray(inputs["subln_g"][l]),
        "a_re": np.ascontiguousarray(inputs["ssm_a_re"][l][:, gs]), "a_im": np.ascontiguousarray(inputs["ssm_a_im"][l][:, gs]),
        "b_re": np.ascontiguousarray(inputs["ssm_b_re"][l][:, gs]), "b_im": np.ascontiguousarray(inputs["ssm_b_im"][l][:, gs]),
        "c_re": np.ascontiguousarray(inputs["ssm_c_re"][l][:, gs]), "c_im": np.ascontiguousarray(inputs["ssm_c_im"][l][:, gs]),
        "log_dt": np.ascontiguousarray(inputs["ssm_log_dt"][l][:, gs]),
        "ssm_d": np.ascontiguousarray(inputs["ssm_d"][l][128 * f:128 * f + 128]),
    }


def t_tokens_T(x_b, ctx_b, j):
    rows = np.concatenate([x_b[2048 * j:2048 * j + 2048], ctx_b[64 * j:64 * j + 64]], axis=0)
    return np.ascontiguousarray(rows.T)


class Caster:
    def __init__(self, B, nc, es, tag, nbuf=2, width=2048):
        self.B = B
        self.st = [sb(nc, es, tag + "st%d" % i, [128, width], F32) for i in range(nbuf)]
        self.r = [Res() for _ in range(nbuf)]
        self.i = 0
        self.width = width
        self.engs = ("act", "pool", "dve")

    def load(self, dst, src, K, N, r_dst):
        B = self.B
        step = min(N, self.width)
        for k in range(K):
            for n0 in range(0, N, step):
                i = self.i
                self.i += 1
                st, r = self.st[i % len(self.st)], self.r[i % len(self.st)]
                B.dma(st[:, :step], src[:, k, n0:n0 + step], writes=[r])
                eng = self.engs[i % 3]
                if eng == "act":
                    B.op("act", lambda h, st=st, k=k, n0=n0: h.activation(dst[:, k, n0:n0 + step], st[:, :step], AF.Copy),
                         reads=[r], writes=[r_dst])
                else:
                    B.op(eng, lambda h, st=st, k=k, n0=n0: h.tensor_copy(dst[:, k, n0:n0 + step], st[:, :step]),
                         reads=[r], writes=[r_dst])


def phase_TT(B, nc, es, C, io, last=False):
    if "fo_all" in io:
        io0 = io

        def emit_dyn():
            B.flush()
            B.op("sp", lambda h: h.dma_start(
                out=io0["fo_own"][:, :],
                in_=io0["fo_all"][bass.ds(h.snap((h.partition_id() % 4) * 1024, min_val=0, max_val=3072), 1024), :]),
                reads=[io0["fo_all_r"]], writes=[io0["fo_own_r"]], dma=True)
            B.flush()
        io = dict(io)
        io["emit_dyn"] = emit_dyn
        fo4 = io["fo_own"].rearrange("(c f p) n -> c f p n", c=2, f=4)
        io["on4"], io["yg4"] = fo4[0], fo4[1]
        io["on4_r"] = io["yg4_r"] = io["fo_own_r"]
    with ExitStack() as s0:
        if "md" in io:
            md, md_r, md2, md2_r = io["md"], io["md_r"], io.get("md2"), io.get("md2_r")
        else:
            md = sb(nc, s0, "tt_md", [128, 6, 8, 2], F32)
            md2 = sb(nc, s0, "tt_md2", [128, 6, 8, 2], F32)
            md_r, md2_r = Res(), Res()
            emit_mods(B, nc, s0, C, md, md_r, io["c_b"], io["c_ctx"], io["ada_w"], io["ada_b"], io["norm_g"], "ttm_")
            if not last:
                emit_mods(B, nc, s0, C, md2, md2_r, io["c_b"], io["c_ctx"], io["ada_w2"], io["ada_b2"], io["norm_g2"], "ttm2_")
        xv = io["xT"].rearrange("(k p) n -> p k n", p=128)
        hv = io["hT"].rearrange("(k p) n -> p k n", p=128)
        xmid = io["xmid"].rearrange("(k p) n -> p k n", p=128)
        h2v = io["h2T"].rearrange("(k p) n -> p k n", p=128)
        xov = io["xT_out"].rearrange("(k p) n -> p k n", p=128)
        hov = io["hT_out"].rearrange("(k p) n -> p k n", p=128)
        with ExitStack() as ls:
            wg = sb(nc, ls, "ta_wg", [128, 8, 2048], BF16)
            wba = sb(nc, ls, "ta_wba", [128, 4, 1024], BF16)
            wgl = sb(nc, ls, "ta_wgl", [128, 4, 512], BF16)
            wbs = sb(nc, ls, "ta_wbs", [128, 4, 1024], BF16)
            wo = sb(nc, ls, "ta_wo", [128, 8, 1024], BF16)
            gb = sb(nc, ls, "ta_gb", [128, 16], F32)
            bgl = sb(nc, ls, "ta_bgl", [128, 4], F32)
            r_wg, r_wba, r_wgl, r_wbs, r_wo, r_bias = (Res() for _ in range(6))
            cast = Caster(B, nc, ls, "ta_c")
            cast.load(wg, io["w_g"].rearrange("(k p) n -> p k n", p=128), 8, 2048, r_wg)
            cast.load(wba, io["w_br_a"].rearrange("(k p) n -> p k n", p=128), 4, 1024, r_wba)
            cast.load(wgl, io["w_glu"].rearrange("(k p) n -> p k n", p=128), 4, 512, r_wgl)
            cast.load(wbs, io["w_br_s"].rearrange("(k p) n -> p k n", p=128), 4, 1024, r_wbs)
            cast.load(wo, io["w_out"].rearrange("(k p) n -> p k n", p=128), 8, 1024, r_wo)
            B.dma(gb[:], io["gate_b"].rearrange("(c p) -> p c", p=128), writes=[r_bias], slow=True)
            B.dma(bgl[:], io["b_glu"].rearrange("(c p) -> p c", p=128), writes=[r_bias], slow=True)
            if "emit_dyn" in io:
                io["emit_dyn"]()
            x = sb(nc, ls, "ta_x", [128, 8, 512], F32)
            hh = sb(nc, ls, "ta_h", [128, 8, 512], BF16)
            on = sb(nc, ls, "ta_on", [128, 4, 512], BF16)
            yg = sb(nc, ls, "ta_yg", [128, 4, 512], BF16)
            ygl = sb(nc, ls, "ta_ygl", [128, 4, 512], BF16)
            sig = [sb(nc, ls, "ta_sig%d" % i, [128, 512], F32) for i in range(2)]
            gas = [sb(nc, ls, "ta_gas%d" % i, [128, 512], F32) for i in range(2)]
            gss = [sb(nc, ls, "ta_gss%d" % i, [128, 512], F32) for i in range(2)]
            m1 = [sb(nc, ls, "ta_m1%d" % i, [128, 512], F32) for i in range(2)]
            m2 = [sb(nc, ls, "ta_m2%d" % i, [128, 512], F32) for i in range(2)]
            mm_ = sb(nc, ls, "ta_m", [128, 8, 512], BF16)
            mixf = sb(nc, ls, "ta_mixf", [128, 8, 512], F32)
            sq = sb(nc, ls, "ta_sq", [128, 8, 512], BF16)
            tmp = sb(nc, ls, "ta_tmp", [128, 2, 512], F32)
            sd = sb(nc, ls, "ta_sd", [128, 512], F32)
            rstd = sb(nc, ls, "ta_rstd", [128, 512], F32)
            h2 = sb(nc, ls, "ta_h2", [128, 8, 512], BF16)
            pA = [ps(nc, ls, "ta_pA%d" % i, [128, 512]) for i in range(2)]
            pB = [ps(nc, ls, "ta_pB%d" % i, [128, 512]) for i in range(2)]
            pC = ps(nc, ls, "ta_pC", [128, 512])
            pD = ps(nc, ls, "ta_pD", [128, 512])
            pE = ps(nc, ls, "ta_pE", [128, 512])
            pS = ps(nc, ls, "ta_pS", [128, 512])
            r_x, r_h, r_on, r_yg, r_ygl, r_m, r_mix, r_sq, r_ss, r_sd, r_rstd, r_h2 = (Res() for _ in range(12))
            r_sig, r_gas, r_gss, r_m1, r_m2 = ([Res(), Res()] for _ in range(5))
            r_pA, r_pB = [Res(), Res()], [Res(), Res()]
            r_pC, r_pD, r_pE = Res(), Res(), Res()
            r_tmp = [Res(), Res()]
            for ci, (t0, n) in enumerate(T_CHUNKS):
                s = 1 if ci == 4 else 0
                B.dma(x[:, :, :n], xv[:, :, t0:t0 + n], writes=[r_x], reads=[io["xT_r"]])
                B.dma(hh[:, :, :n], hv[:, :, t0:t0 + n], writes=[r_h], reads=[io["hT_r"]])
                B.dma(on[:, :, :n], io["on4"].rearrange("f p n -> p f n")[:, :, t0:t0 + n], writes=[r_on], reads=[io["on4_r"]])
                B.dma(yg[:, :, :n], io["yg4"].rearrange("f p n -> p f n")[:, :, t0:t0 + n], writes=[r_yg], reads=[io["yg4_r"]])
                for t4 in range(4):
                    def mmg(h, t4=t4, n=n):
                        ins = None
                        for k in range(4):
                            ins = h.matmul(pE[:, :n], wgl[:, k, t4 * 128:(t4 + 1) * 128], yg[:, k, :n], start=(k == 0), stop=(k == 3))
                        return ins
                    B.op("pe", mmg, reads=[r_wgl, r_yg], writes=[r_pE])
                    B.op("act", lambda h, t4=t4, n=n: h.activation(sig[t4 % 2][:, :n], pE[:, :n], AF.Sigmoid, bias=bgl[:, t4:t4 + 1]),
                         reads=[r_pE, r_bias], writes=[r_sig[t4 % 2]])
                    B.op("pool", lambda h, t4=t4, n=n: h.tensor_tensor(ygl[:, t4, :n], yg[:, t4, :n], sig[t4 % 2][:, :n], ALU.mult),
                         reads=[r_yg, r_sig[t4 % 2]], writes=[r_ygl])
                for ft in range(8):
                    b2_ = ft % 2

                    def mma(h, ft=ft, n=n, b2_=b2_):
                        ins = None
                        for k in range(8):
                            ins = h.matmul(pA[b2_][:, :n], wg[:, k, ft * 128:(ft + 1) * 128], hh[:, k, :n], start=(k == 0), stop=(k == 7))
                        return ins

                    def mmb(h, ft=ft, n=n, b2_=b2_):
                        ins = None
                        for k in range(8):
                            ins = h.matmul(pB[b2_][:, :n], wg[:, k, 1024 + ft * 128:1024 + (ft + 1) * 128], hh[:, k, :n], start=(k == 0), stop=(k == 7))
                        return ins

                    def mmc(h, ft=ft, n=n):
                        ins = None
                        for k in range(4):
                            ins = h.matmul(pC[:, :n], wba[:, k, ft * 128:(ft + 1) * 128], on[:, k, :n], start=(k == 0), stop=(k == 3))
                        return ins

                    def mmd(h, ft=ft, n=n):
                        ins = None
                        for k in range(4):
                            ins = h.matmul(pD[:, :n], wbs[:, k, ft * 128:(ft + 1) * 128], ygl[:, k, :n], start=(k == 0), stop=(k == 3))
                        return ins
                    B.op("pe", mma, reads=[r_wg, r_h], writes=[r_pA[b2_]])
                    B.op("pe", mmb, reads=[r_wg, r_h], writes=[r_pB[b2_]])
                    B.op("pe", mmc, reads=[r_wba, r_on], writes=[r_pC])
                    B.op("pe", mmd, reads=[r_wbs, r_ygl], writes=[r_pD])
                    B.op("act", lambda h, ft=ft, n=n, b2_=b2_: h.activation(gas[b2_][:, :n], pA[b2_][:, :n], AF.Sigmoid, bias=gb[:, ft:ft + 1]),
                         reads=[r_pA[b2_], r_bias], writes=[r_gas[b2_]])
                    B.op("act", lambda h, ft=ft, n=n, b2_=b2_: h.activation(gss[b2_][:, :n], pB[b2_][:, :n], AF.Sigmoid, bias=gb[:, 8 + ft:9 + ft]),
                         reads=[r_pB[b2_], r_bias], writes=[r_gss[b2_]])
                    B.op("dve", lambda h, n=n, b2_=b2_: h.tensor_tensor(m1[b2_][:, :n], pC[:, :n], gas[b2_][:, :n], ALU.mult),
                         reads=[r_pC, r_gas[b2_]], writes=[r_m1[b2_]])
                    B.op("dve", lambda h, n=n, b2_=b2_: h.tensor_tensor(m2[b2_][:, :n], pD[:, :n], gss[b2_][:, :n], ALU.mult),
                         reads=[r_pD, r_gss[b2_]], writes=[r_m2[b2_]])
                    B.op("pool", lambda h, ft=ft, n=n, b2_=b2_: h.tensor_tensor(mm_[:, ft, :n], m1[b2_][:, :n], m2[b2_][:, :n], ALU.add),
                         reads=[r_m1[b2_], r_m2[b2_]], writes=[r_m])
                for ft in range(8):
                    def mmo(h, ft=ft, n=n):
                        ins = None
                        for k in range(8):
                            ins = h.matmul(pA[ft % 2][:, :n], wo[:, k, ft * 128:(ft + 1) * 128], mm_[:, k, :n], start=(k == 0), stop=(k == 7))
                        return ins
                    B.op("pe", mmo, reads=[r_wo, r_m], writes=[r_pA[ft % 2]])
                    B.op("act", lambda h, ft=ft, n=n: h.activation(mixf[:, ft, :n], pA[ft % 2][:, :n], AF.Copy),
                         reads=[r_pA[ft % 2]], writes=[r_mix])
                emit_rstd(B, C, mixf, n, sq, pS, sd, rstd, r_mix, r_sq, r_ss, r_sd, r_rstd)
                for k in range(8):
                    B.op("dve", lambda h, k=k, n=n, s=s: h.scalar_tensor_tensor(tmp[:, k % 2, :n], mixf[:, k, :n], md[:, 2, k:k + 1, s], rstd[:, :n],
                                                                               ALU.mult, ALU.mult),
                         reads=[r_mix, r_rstd, md_r], writes=[r_tmp[k % 2]])
                    B.op("pool", lambda h, k=k, n=n: h.tensor_tensor(x[:, k, :n], x[:, k, :n], tmp[:, k % 2, :n], ALU.add),
                         reads=[r_tmp[k % 2], r_x], writes=[r_x])
                B.dma(xmid[:, :, t0:t0 + n], x[:, :, :n], reads=[r_x], writes=[io["xmid_r"]])
                emit_rstd(B, C, x, n, sq, pS, sd, rstd, r_x, r_sq, r_ss, r_sd, r_rstd)
                emit_norm_mod(B, x, n, rstd, md[:, 4, :, s], md[:, 3, :, s], h2, tmp, r_x, r_rstd, md_r, r_tmp, r_h2)
                B.dma(h2v[:, :, t0:t0 + n], h2[:, :, :n], reads=[r_h2], writes=[io["h2T_r"]])
            B.flush()
        s1 = s0.enter_context(ExitStack())
        facc = sb(nc, s1, "tb_facc", [128, 8, NT], F32)
        r_facc = Res()
        with ExitStack() as ls:
            h2a = sb(nc, ls, "tb_h2", [128, 8, NT], BF16)
            w1 = [sb(nc, ls, "tb_w1%d" % i, [128, 8, 512], BF16) for i in range(2)]
            w2 = [sb(nc, ls, "tb_w2%d" % i, [128, 4, 1024], BF16) for i in range(2)]
            b1 = sb(nc, ls, "tb_b1", [128, 32], F32)
            b2 = sb(nc, ls, "tb_b2", [128, 8], F32)
            rl = [sb(nc, ls, "tb_rl%d" % i, [128, 512], F32) for i in range(2)]
            aa = [sb(nc, ls, "tb_a%d" % i, [128, 4, 512], BF16) for i in range(2)]
            pH = [ps(nc, ls, "tb_pH%d" % i, [128, 512]) for i in range(3)]
            pF = [ps(nc, ls, "tb_pF%d" % i, [128, 512]) for i in range(3)]
            r_h2a, r_b = Res(), Res()
            r_w1, r_w2 = [Res(), Res()], [Res(), Res()]
            r_rl, r_aa = [Res(), Res()], [Res(), Res()]
            r_pH, r_pF = [Res() for _ in range(3)], [Res() for _ in range(3)]
            cast = Caster(B, nc, ls, "tb_c")
            B.dma(h2a[:], h2v, reads=[io["h2T_r"]], writes=[r_h2a])
            B.dma(b1[:], io["b_mlp1"].rearrange("(c p) -> p c", p=128), writes=[r_b], slow=True)
            B.dma(b2[:], io["b_mlp2"].rearrange("(c p) -> p c", p=128), writes=[r_b], slow=True)
            w1v = io["w_mlp1"].rearrange("(k p) n -> p k n", p=128)
            w2v = io["w_mlp2"].rearrange("(k p) n -> p k n", p=128)
            ih, i_f, ia = 0, 0, 0
            for hb in range(8):
                wb1, wb2 = w1[hb % 2], w2[hb % 2]
                cast.load(wb1, w1v[:, :, hb * 512:(hb + 1) * 512], 8, 512, r_w1[hb % 2])
                cast.load(wb2, w2v[:, 4 * hb:4 * hb + 4, :], 4, 1024, r_w2[hb % 2])
                for ci, (t0, n) in enumerate(T_CHUNKS):
                    a_ = aa[ia % 2]
                    ra = r_aa[ia % 2]
                    ia += 1
                    for t4 in range(4):
                        ph, rph = pH[ih % 3], r_pH[ih % 3]
                        rr, rrl = rl[ih % 2], r_rl[ih % 2]
                        ih += 1

                        def mm1(h, ph=ph, wb1=wb1, t4=t4, t0=t0, n=n):
                            ins = None
                            for k in range(8):
                                ins = h.matmul(ph[:, :n], wb1[:, k, t4 * 128:(t4 + 1) * 128], h2a[:, k, t0:t0 + n], start=(k == 0), stop=(k == 7))
                            return ins
                        B.op("pe", mm1, reads=[r_w1[hb % 2], r_h2a], writes=[rph])
                        B.op("act", lambda h, ph=ph, rr=rr, hb=hb, t4=t4, n=n: h.activation(rr[:, :n], ph[:, :n], AF.Relu, bias=b1[:, 4 * hb + t4:4 * hb + t4 + 1]),
                             reads=[rph, r_b], writes=[rrl])
                        B.op("pool", lambda h, rr=rr, a_=a_, t4=t4, n=n: h.tensor_tensor(a_[:, t4, :n], rr[:, :n], rr[:, :n], ALU.mult),
                             reads=[rrl], writes=[ra])
                    for ft in range(8):
                        pf, rpf = pF[i_f % 3], r_pF[i_f % 3]
                        i_f += 1

                        def mm2(h, pf=pf, wb2=wb2, a_=a_, ft=ft, n=n):
                            ins = None
                            for k in range(4):
                                ins = h.matmul(pf[:, :n], wb2[:, k, ft * 128:(ft + 1) * 128], a_[:, k, :n], start=(k == 0), stop=(k == 3))
                            return ins
                        B.op("pe", mm2, reads=[r_w2[hb % 2], ra], writes=[rpf])
                        if hb == 0:
                            B.op("act", lambda h, pf=pf, ft=ft, t0=t0, n=n: h.activation(facc[:, ft, t0:t0 + n], pf[:, :n], AF.Identity, bias=b2[:, ft:ft + 1]),
                                 reads=[rpf, r_b], writes=[r_facc])
                        else:
                            B.op("dve", lambda h, pf=pf, ft=ft, t0=t0, n=n: h.tensor_tensor(facc[:, ft, t0:t0 + n], facc[:, ft, t0:t0 + n], pf[:, :n], ALU.add),
                                 reads=[rpf, r_facc], writes=[r_facc])
            B.flush()
        with ExitStack() as ls:
            x = sb(nc, ls, "tc_x", [128, 8, 512], F32)
            sq = sb(nc, ls, "tc_sq", [128, 8, 512], BF16)
            tmp = sb(nc, ls, "tc_tmp", [128, 2, 512], F32)
            sd = sb(nc, ls, "tc_sd", [128, 512], F32)
            rstd = sb(nc, ls, "tc_rstd", [128, 512], F32)
            hn = sb(nc, ls, "tc_hn", [128, 8, 512], BF16)
            fch = sb(nc, ls, "tc_f", [128, 8, 512], F32)
            pS = ps(nc, ls, "tc_pS", [128, 512])
            r_x, r_sq, r_ss, r_sd, r_rstd, r_hn, r_f = (Res() for _ in range(7))
            r_tmp = [Res(), Res()]
            for ci, (t0, n) in enumerate(T_CHUNKS):
                s = 1 if ci == 4 else 0
                B.dma(x[:, :, :n], xmid[:, :, t0:t0 + n], reads=[io["xmid_r"]], writes=[r_x])
                B.op("pool", lambda h, t0=t0, n=n: h.tensor_copy(fch[:, :, :n], facc[:, :, t0:t0 + n]), reads=[r_facc], writes=[r_f])
                emit_rstd(B, C, fch, n, sq, pS, sd, rstd, r_f, r_sq, r_ss, r_sd, r_rstd)
                for k in range(8):
                    B.op("dve", lambda h, k=k, n=n, s=s: h.scalar_tensor_tensor(tmp[:, k % 2, :n], fch[:, k, :n], md[:, 5, k:k + 1, s], rstd[:, :n],
                                                                               ALU.mult, ALU.mult),
                         reads=[r_f, r_rstd, md_r], writes=[r_tmp[k % 2]])
                    B.op("pool", lambda h, k=k, n=n: h.tensor_tensor(x[:, k, :n], x[:, k, :n], tmp[:, k % 2, :n], ALU.add),
                         reads=[r_tmp[k % 2], r_x], writes=[r_x])
                B.dma(xov[:, :, t0:t0 + n], x[:, :, :n], reads=[r_x], writes=[io["xT_out_r"]])
                if not last:
                    emit_rstd(B, C, x, n, sq, pS, sd, rstd, r_x, r_sq, r_ss, r_sd, r_rstd)
                    emit_norm_mod(B, x, n, rstd, md2[:, 1, :, s], md2[:, 0, :, s], hn, tmp, r_x, r_rstd, md2_r, r_tmp, r_hn)
                    B.dma(hov[:, :, t0:t0 + n], hn[:, :, :n], reads=[r_hn], writes=[io["hT_out_r"]])
            B.barrier("sp", [io["xT_out_r"], io["hT_out_r"]])
            B.flush()


def build_TT():
    nc = bass.Bass("TRN2", target_bir_lowering=False)
    DBG["nc"] = nc
    io = {
        "xT": dram_in(nc, "xT", [D, NT], F32), "hT": dram_in(nc, "hT", [D, NT], BF16),
        "xT_r": Res(), "hT_r": Res(),
        "on4": dram_in(nc, "on4", [4, 128, NT], BF16), "on4_r": Res(),
        "yg4": dram_in(nc, "yg4", [4, 128, NT], BF16), "yg4_r": Res(),
        "c_b": dram_in(nc, "c_b", [D], F32), "c_ctx": dram_in(nc, "c_ctx", [D], F32),
        "ada_w": dram_in(nc, "ada_w", [D, 6 * D], F32), "ada_b": dram_in(nc, "ada_b", [6 * D], F32),
        "norm_g": dram_in(nc, "norm_g", [4, D], F32),
        "ada_w2": dram_in(nc, "ada_w2", [D, 6 * D], F32), "ada_b2": dram_in(nc, "ada_b2", [6 * D], F32),
        "norm_g2": dram_in(nc, "norm_g2", [4, D], F32),
        "w_g": dram_in(nc, "w_g", [D, 2048], F32), "gate_b": dram_in(nc, "gate_b", [2048], F32),
        "w_br_a": dram_in(nc, "w_br_a", [512, D], F32), "w_glu": dram_in(nc, "w_glu", [512, 512], F32),
        "b_glu": dram_in(nc, "b_glu", [512], F32), "w_br_s": dram_in(nc, "w_br_s", [512, D], F32),
        "w_out": dram_in(nc, "w_out", [D, D], F32), "w_mlp1": dram_in(nc, "w_mlp1", [D, 4096], F32),
        "b_mlp1": dram_in(nc, "b_mlp1", [4096], F32), "w_mlp2": dram_in(nc, "w_mlp2", [4096, D], F32),
        "b_mlp2": dram_in(nc, "b_mlp2", [D], F32),
        "xmid": nc.dram_tensor("xmid", [D, NT], F32).ap(), "xmid_r": Res(),
        "h2T": nc.dram_tensor("h2T", [D, NT], BF16).ap(), "h2T_r": Res(),
        "xT_out": dram_out(nc, "xT_out", [D, NT], F32), "xT_out_r": Res(),
        "hT_out": dram_out(nc, "hT_out", [D, NT], BF16), "hT_out_r": Res(),
    }
    with ExitStack() as es:
        B = Builder(nc, es)
        C = make_consts(B, nc, es)
        phase_TT(B, nc, es, C, io)
    return nc


def tt_inputs(l, b, xT, hT, on4, yg4, inputs):
    l2 = min(l + 1, inputs["ada_w"].shape[0] - 1)
    c = np.ascontiguousarray
    return {
        "xT": xT, "hT": hT, "on4": on4, "yg4": yg4,
        "c_b": c(inputs["c"][b]), "c_ctx": c(inputs["c_ctx"]),
        "ada_w": c(inputs["ada_w"][l]), "ada_b": c(inputs["ada_b"][l]), "norm_g": c(inputs["norm_g"][l]),
        "ada_w2": c(inputs["ada_w"][l2]), "ada_b2": c(inputs["ada_b"][l2]), "norm_g2": c(inputs["norm_g"][l2]),
        "w_g": c(inputs["w_in"][l][:, 2048:4096]), "gate_b": c(inputs["gate_b"][l]),
        "w_br_a": c(inputs["w_br_a"][l]), "w_glu": c(inputs["w_glu"][l]), "b_glu": c(inputs["b_glu"][l]),
        "w_br_s": c(inputs["w_br_s"][l]), "w_out": c(inputs["w_out"][l]), "w_mlp1": c(inputs["w_mlp1"][l]),
        "b_mlp1": c(inputs["b_mlp1"][l]), "w_mlp2": c(inputs["w_mlp2"][l]), "b_mlp2": c(inputs["b_mlp2"][l]),
    }


_PROGS = {}


def _prog(name):
    if name not in _PROGS:
        _PROGS[name] = {"T0": build_T0, "F": build_F, "TT": build_TT}[name]()
    return _PROGS[name]


def kernel_unfused(**inputs):
    inputs = {k: np.asarray(v) for k, v in inputs.items()}
    cores = list(range(8))
    x, ctx = inputs["x"], inputs["ctx"]
    xT = [t_tokens_T(x[i // 4], ctx[i // 4], i % 4) for i in cores]
    c = np.ascontiguousarray
    maps = [{"xT": xT[i], "c_b": c(inputs["c"][i // 4]), "c_ctx": c(inputs["c_ctx"]), "ada_w": c(inputs["ada_w"][0]),
             "ada_b": c(inputs["ada_b"][0]), "norm_g": c(inputs["norm_g"][0])} for i in cores]
    res = run_bass_kernel_spmd(_prog("T0"), maps, core_ids=cores)
    hT = [np.asarray(r["hT_out"]) for r in res.results]
    for l in range(2):
        maps = []
        for i in cores:
            b, f = i // 4, i % 4
            maps.append(f_inputs(l, b, f, c(np.stack(hT[4 * b:4 * b + 4])), inputs))
        res = run_bass_kernel_spmd(_prog("F"), maps, core_ids=cores)
        on = [np.asarray(r["onT_out"]) for r in res.results]
        yg = [np.asarray(r["ygT_out"]) for r in res.results]
        maps = []
        for i in cores:
            b, j = i // 4, i % 4
            on4 = c(np.stack([on[4 * b + f][j] for f in range(4)]))
            yg4 = c(np.stack([yg[4 * b + f][j] for f in range(4)]))
            maps.append(tt_inputs(l, b, xT[i], hT[i], on4, yg4, inputs))
        res = run_bass_kernel_spmd(_prog("TT"), maps, core_ids=cores)
        xT = [np.asarray(r["xT_out"]) for r in res.results]
        hT = [np.asarray(r["hT_out"]) for r in res.results]
    out = np.empty((2, SEQ, D), np.float32)
    for i in cores:
        b, j = i // 4, i % 4
        out[b, 2048 * j:2048 * j + 2048] = xT[i][:, :2048].T
    return out


RG = [[0, 1, 2, 3], [4, 5, 6, 7]]
FLAGS = {"no_cc": False, "no_dyn": False, "stop_after": 99}
F_IN = [("w6", [D, 768]), ("lam_qk", [4, 64]), ("lcon", [1, 4]), ("subln_g", [128])]
S5_IN = [("a_re", [2, 8, 64]), ("a_im", [2, 8, 64]), ("b_re", [2, 8, 64, 16]), ("b_im", [2, 8, 64, 16]),
         ("c_re", [2, 8, 16, 64]), ("c_im", [2, 8, 16, 64]), ("log_dt", [2, 8]), ("ssm_d", [128])]
TT_IN = [("ada_w", [D, 6 * D]), ("ada_b", [6 * D]), ("norm_g", [4, D]), ("w_g", [D, 2048]), ("gate_b", [2048]),
         ("w_br_a", [512, D]), ("w_glu", [512, 512]), ("b_glu", [512]), ("w_br_s", [512, D]), ("w_out", [D, D]),
         ("w_mlp1", [D, 4096]), ("b_mlp1", [4096]), ("w_mlp2", [4096, D]), ("b_mlp2", [D])]


def all_gather(B, src, dst, nrows, r_src, r_dst, pieces=None):
    for q in (range(nrows // 128) if pieces is None else pieces):
        B.op("pool", lambda h, q=q: h.collective_compute(
            "AllGather", ALU.bypass, replica_groups=RG,
            ins=[src[128 * q:128 * q + 128, :]], outs=[dst[512 * q:512 * q + 512, :]]),
            reads=[r_src], writes=[r_dst], cc=True)


def build_fused():
    _UNIQ[0] = 0
    nc = bass.Bass("TRN2", target_bir_lowering=False)
    DBG["nc"] = nc
    DBG["on"] = False
    xT = dram_in(nc, "xT", [D, NT], F32)
    c_b = dram_in(nc, "c_b", [D], F32)
    c_ctx = dram_in(nc, "c_ctx", [D], F32)
    ropec = dram_in(nc, "ropec", [128, 2], F32)
    fin = [{nm: dram_in(nc, "%s_%d" % (nm, l), shp, F32) for nm, shp in F_IN + S5_IN} for l in range(2)]
    tin = [{nm: dram_in(nc, "%s_%d" % (nm, l), shp, F32) for nm, shp in TT_IN} for l in range(2)]
    hT_own = nc.dram_tensor("hT_own", [D, NT], BF16).ap()
    hT_all = nc.dram_tensor("hT_all", [4 * D, NT], BF16).ap()
    fo = nc.dram_tensor("fo", [4 * 256, NT], BF16).ap()
    fo_all = nc.dram_tensor("fo_all", [16 * 256, NT], BF16).ap()
    xres = nc.dram_tensor("xres", [D, NT], F32).ap()
    xmid = nc.dram_tensor("xmid", [D, NT], F32).ap()
    h2T = nc.dram_tensor("h2T", [D, NT], BF16).ap()
    fo_own = nc.dram_tensor("fo_own", [4 * 256, NT], BF16).ap()
    r_foown = Res()
    xT_out = dram_out(nc, "xT_out", [D, NT], F32)
    r_hown, r_hall, r_fo, r_foall, r_xres, r_xmid, r_h2T, r_xout, r_none = (Res() for _ in range(9))
    r_fo_on = Res()
    with ExitStack() as es:
        B = Builder(nc, es)
        C = make_consts(B, nc, es)
        mds = [sb(nc, es, "md_l%d" % l, [128, 6, 8, 2], F32) for l in range(2)]
        md_rs = [Res(), Res()]
        emit_mods(B, nc, es, C, mds[0], md_rs[0], c_b, c_ctx, tin[0]["ada_w"], tin[0]["ada_b"], tin[0]["norm_g"], "m0_")
        with ExitStack() as t0s:
            phase_T0(B, nc, t0s, C, {"xT": xT, "md": mds[0], "md_r": md_rs[0], "hT_out": hT_own, "hT_out_r": r_hown},
                     defer=True)
            all_gather(B, hT_own, hT_all, D, r_hown, r_hall)
            emit_mods(B, nc, t0s, C, mds[1], md_rs[1], c_b, c_ctx, tin[1]["ada_w"], tin[1]["ada_b"], tin[1]["norm_g"], "m1_")
        fo3 = fo.rearrange("(j c) n -> j c n", j=4)
        step = [0]

        def stop():
            step[0] += 1
            return step[0] > FLAGS["stop_after"]

        def finish():
            B.dma(xT_out[0:128, 0:64], xT[0:128, 0:64], writes=[r_xout])
            B.barrier("sp", [r_xout])
            B.flush()
        for l in range(2):
            if stop():
                finish()
                return nc
            if l > 0:
                all_gather(B, hT_own, hT_all, D, r_hown, r_hall)
            if stop():
                finish()
                return nc
            hv4 = hT_all.rearrange("(k r p) n -> r p k n", k=8, r=4, p=128)
            fio = {"hT_pk": [hv4[r] for r in range(4)], "hT_all_r": r_hall, "ropec": ropec,
                   "w6": fin[l]["w6"], "lam_qk": fin[l]["lam_qk"], "lcon": fin[l]["lcon"], "subln_g": fin[l]["subln_g"],
                   "s5p": {"a_re": fin[l]["a_re"], "a_im": fin[l]["a_im"], "b_re": fin[l]["b_re"], "b_im": fin[l]["b_im"],
                           "c_re": fin[l]["c_re"], "c_im": fin[l]["c_im"], "log_dt": fin[l]["log_dt"], "d": fin[l]["ssm_d"]},
                   "onT_out": fo3[:, 0:128, :], "onT_out_r": r_fo_on, "ygT_out": fo3[:, 128:256, :], "ygT_out_r": r_fo}
            phase_F(B, nc, es, C, fio, None,
                    after_attn=lambda: all_gather(B, fo, fo_all, 1024, r_fo_on, r_foall, pieces=[0, 2, 4, 6]))
            if stop():
                finish()
                return nc
            all_gather(B, fo, fo_all, 1024, r_fo, r_foall, pieces=[1, 3, 5, 7])
            if stop():
                finish()
                return nc
            last = (l == 1)
            tio = dict(tin[l])
            tio.update({"xT": xT if l == 0 else xres, "xT_r": r_none if l == 0 else r_xres,
                        "hT": hT_own, "hT_r": r_hown, "fo_all": fo_all, "fo_all_r": r_foall, "fo_own": fo_own, "fo_own_r": r_foown,
                        "c_b": c_b, "c_ctx": c_ctx,
                        "xmid": xmid, "xmid_r": r_xmid, "h2T": h2T, "h2T_r": r_h2T,
                        "xT_out": xT_out if last else xres, "xT_out_r": r_xout if last else r_xres,
                        "hT_out": hT_own, "hT_out_r": r_hown})
            tio.update({"md": mds[l], "md_r": md_rs[l]})
            if not last:
                tio.update({"md2": mds[1], "md2_r": md_rs[1]})
            phase_TT(B, nc, es, C, tio, last=last)
    return nc


def fused_inputs(i, inputs):
    b, j = i // 4, i % 4
    c = np.ascontiguousarray
    m = {"xT": t_tokens_T(inputs["x"][b], inputs["ctx"][b], j), "c_b": c(inputs["c"][b]), "c_ctx": c(inputs["c_ctx"]),
         "ropec": rope_consts()}
    for l in range(2):
        fi = f_inputs(l, b, j, None, inputs)
        for nm, _ in F_IN + S5_IN:
            m["%s_%d" % (nm, l)] = fi[nm]
        ti = tt_inputs(l, b, None, None, None, None, inputs)
        for nm, _ in TT_IN:
            m["%s_%d" % (nm, l)] = ti[nm]
    return m


def kernel(**inputs):
    inputs = {k: np.asarray(v) for k, v in inputs.items()}
    cores = list(range(8))
    if "FUSED" not in _PROGS:
        _PROGS["FUSED"] = build_fused()
    maps = [fused_inputs(i, inputs) for i in cores]
    res = run_bass_kernel_spmd(_PROGS["FUSED"], maps, core_ids=cores)
    out = np.empty((2, SEQ, D), np.float32)
    for i in cores:
        b, j = i // 4, i % 4
        out[b, 2048 * j:2048 * j + 2048] = np.asarray(res.results[i]["xT_out"])[:, :2048].T
    return out
```
